# Optimizing a Trainium2 kernel written in Bass

```python
import jax, jax.numpy as jnp
from jax import lax
import numpy as np

D_MODEL = 1024
BATCH = 8
SEQ = 2048
DEPTH = 4

N_A_LAYERS = DEPTH // 2
N_B_LAYERS = DEPTH - N_A_LAYERS
HEAD_DIM = 64
SB_HEADS = D_MODEL // HEAD_DIM
NSA_HEADS = D_MODEL // HEAD_DIM
NSA_KV_HEADS = 4
NSA_GROUP = NSA_HEADS // NSA_KV_HEADS
N_BRANCHES = 3
ROPE_DIMS = HEAD_DIM // 4
ROPE_THETA = 500000.0
D_FF = -(-8 * D_MODEL // (3 * 256)) * 256
Q_BLOCK = 128
SEL_Q_BLOCK = 64
CMP_BLOCK = 32
CMP_STRIDE = 16
CMP_HIDDEN = 4 * HEAD_DIM
SEL_BLOCK = 64
N_SEL = 8
N_LOCAL_SEL = 2
WINDOW = 512
EPS = 1e-6
NEG = -1e30
FORCE = 1e4

kernel_name = 'hybrid_stickbreak_nsa_yoco'


def _rmsnorm(x, g):
    x32 = x.astype(jnp.float32)
    y = x32 * lax.rsqrt(jnp.mean(x32 * x32, axis=-1, keepdims=True) + EPS)
    return (y * g.astype(jnp.float32)).astype(x.dtype)


def _swiglu(x, w_in, w_out):
    a, b = jnp.split(x @ w_in, 2, axis=-1)
    return (jax.nn.silu(a) * b) @ w_out


def _rope_tables(positions):
    inv_freq = jnp.power(ROPE_THETA, -jnp.arange(0, ROPE_DIMS, 2, dtype=jnp.float32) / ROPE_DIMS)
    ang = positions.astype(jnp.float32)[:, None] * inv_freq[None, :]
    return jnp.cos(ang), jnp.sin(ang)


def _apply_rope(x, cos, sin):
    half = ROPE_DIMS // 2
    x32 = x.astype(jnp.float32)
    x1, x2 = x32[..., :half], x32[..., half:ROPE_DIMS]
    c, s = cos[:, None, :], sin[:, None, :]
    return jnp.concatenate([x1 * c - x2 * s, x2 * c + x1 * s, x32[..., ROPE_DIMS:]], axis=-1).astype(x.dtype)


def _stick_breaking_attention(q, k, v):
    B, S, H, Dh = q.shape
    scale = Dh ** -0.5
    outs = []
    for start in range(0, S, Q_BLOCK):
        end = start + Q_BLOCK
        z = jnp.einsum('bthd,bshd->bhts', q[:, start:end], k[:, :end], preferred_element_type=jnp.float32) * scale
        t_pos = start + jnp.arange(Q_BLOCK)[:, None]
        s_pos = jnp.arange(end)[None, :]
        past = s_pos < t_pos
        log_1m = jnp.where(past, jax.nn.log_sigmoid(-z), 0.0)
        later = lax.cumsum(log_1m, axis=3, reverse=True) - log_1m
        w = jnp.where(past, jnp.exp(jax.nn.log_sigmoid(z) + later), 0.0)
        outs.append(jnp.einsum('bhts,bshd->bthd', w.astype(v.dtype), v[:, :end]))
    return jnp.concatenate(outs, axis=1)


def _stick_breaking_mixer(hn, w_qkv, w_out):
    B, S, _ = hn.shape
    qkv = (hn @ w_qkv).reshape(B, S, 3, SB_HEADS, HEAD_DIM)
    o = _stick_breaking_attention(qkv[:, :, 0], qkv[:, :, 1], qkv[:, :, 2])
    return o.reshape(B, S, SB_HEADS * HEAD_DIM) @ w_out


def _compress(x, pos_emb, w1, w2):
    B, S, G, Dh = x.shape
    n_cmp = (S - CMP_BLOCK) // CMP_STRIDE + 1
    idx = np.arange(n_cmp)[:, None] * CMP_STRIDE + np.arange(CMP_BLOCK)[None, :]
    blocks = x[:, idx] + pos_emb[:, None, :]
    blocks = blocks.transpose(0, 1, 3, 2, 4).reshape(B, n_cmp, G, CMP_BLOCK * Dh)
    return jax.nn.silu(blocks @ w1) @ w2


def _nsa_shared_kv(h, kv_norm, w_kv, k_norm, pos_k, pos_v, k_w1, k_w2, v_w1, v_w2, cos, sin):
    B, S, _ = h.shape
    kv = (_rmsnorm(h, kv_norm) @ w_kv).reshape(B, S, 2 * N_BRANCHES, NSA_KV_HEADS, HEAD_DIM)
    k_c_raw, v_c_raw, k_s, v_s, k_w, v_w = [kv[:, :, i] for i in range(2 * N_BRANCHES)]
    n_cmp = (S - CMP_BLOCK) // CMP_STRIDE + 1
    cc, sc = _rope_tables(jnp.arange(n_cmp) * CMP_STRIDE + (CMP_BLOCK - 1))
    k_c = _apply_rope(_rmsnorm(_compress(k_c_raw, pos_k, k_w1, k_w2), k_norm[0]), cc, sc)
    v_c = _compress(v_c_raw, pos_v, v_w1, v_w2)
    k_s = _apply_rope(_rmsnorm(k_s, k_norm[1]), cos, sin)
    k_w = _apply_rope(_rmsnorm(k_w, k_norm[2]), cos, sin)
    return (k_c, v_c, k_s, v_s, k_w, v_w)


def _nsa_selected(q, sel_idx, k_sel, v_sel):
    B, S, G, R, Dh = q.shape
    n_k = sel_idx.shape[-1]
    L = n_k * SEL_BLOCK
    scale = Dh ** -0.5
    nblk = S // SEL_Q_BLOCK
    kT = jnp.transpose(k_sel, (0, 2, 1, 3))
    vT = jnp.transpose(v_sel, (0, 2, 1, 3))
    b_ar = jnp.arange(B)[:, None, None, None]
    g_ar = jnp.arange(G)[None, :, None, None]
    offs = jnp.arange(SEL_BLOCK)
    q_blk = q.reshape(B, nblk, SEL_Q_BLOCK, G, R, Dh).transpose(1, 0, 2, 3, 4, 5)
    i_blk = sel_idx.reshape(B, G, nblk, SEL_Q_BLOCK, n_k).transpose(2, 0, 1, 3, 4)
    starts = jnp.arange(nblk) * SEL_Q_BLOCK

    def one_block(args):
        qb, ib, t0 = args
        tok = (ib[..., None] * SEL_BLOCK + offs).reshape(B, G, SEL_Q_BLOCK, L)
        kg = kT[b_ar, g_ar, tok]
        vg = vT[b_ar, g_ar, tok]
        s = jnp.einsum('btgrd,bgtld->bgrtl', qb, kg, preferred_element_type=jnp.float32) * scale
        t_pos = t0 + jnp.arange(SEL_Q_BLOCK)
        mask = tok[:, :, None] <= t_pos[:, None]
        p = jax.nn.softmax(jnp.where(mask, s, NEG), axis=-1)
        return jnp.einsum('bgrtl,bgtld->btgrd', p.astype(vg.dtype), vg)

    out = lax.map(one_block, (q_blk, i_blk, starts))
    return out.transpose(1, 0, 2, 3, 4, 5).reshape(B, S, G, R, Dh)


def _nsa_window(q, k_win, v_win):
    B, S, G, R, Dh = q.shape
    scale = Dh ** -0.5
    nblk = S // Q_BLOCK
    span = WINDOW + Q_BLOCK
    k_pad = jnp.pad(k_win, ((0, 0), (WINDOW, 0), (0, 0), (0, 0)))
    v_pad = jnp.pad(v_win, ((0, 0), (WINDOW, 0), (0, 0), (0, 0)))
    q_blk = q.reshape(B, nblk, Q_BLOCK, G, R, Dh).transpose(1, 0, 2, 3, 4, 5)
    starts = jnp.arange(nblk) * Q_BLOCK

    def one_block(args):
        qb, t0 = args
        kb = lax.dynamic_slice_in_dim(k_pad, t0, span, axis=1)
        vb = lax.dynamic_slice_in_dim(v_pad, t0, span, axis=1)
        s = jnp.einsum('btgrd,bsgd->bgrts', qb, kb, preferred_element_type=jnp.float32) * scale
        t_pos = t0 + jnp.arange(Q_BLOCK)[:, None]
        s_pos = t0 - WINDOW + jnp.arange(span)[None, :]
        mask = (s_pos <= t_pos) & (t_pos - s_pos < WINDOW) & (s_pos >= 0)
        p = jax.nn.softmax(jnp.where(mask, s, NEG), axis=-1)
        return jnp.einsum('bgrts,bsgd->btgrd', p.astype(vb.dtype), vb)

    out = lax.map(one_block, (q_blk, starts))
    return out.transpose(1, 0, 2, 3, 4, 5).reshape(B, S, G, R, Dh)


def _nsa_mixer(hn, kvs, w_in, q_norm, w_out, cos, sin):
    k_c, v_c, k_s, v_s, k_w, v_w = kvs
    B, S, _ = hn.shape
    G, R, Dh = NSA_KV_HEADS, NSA_GROUP, HEAD_DIM
    scale = Dh ** -0.5
    proj = hn @ w_in
    q = proj[..., :NSA_HEADS * Dh].reshape(B, S, NSA_HEADS, Dh)
    q = _apply_rope(_rmsnorm(q, q_norm), cos, sin).reshape(B, S, G, R, Dh)
    gates = jax.nn.sigmoid(proj[..., NSA_HEADS * Dh:].astype(jnp.float32)).reshape(B, S, N_BRANCHES, G, R)
    t_pos = jnp.arange(S)
    n_cmp = k_c.shape[1]
    s_c = jnp.einsum('btgrd,bcgd->bgrtc', q, k_c, preferred_element_type=jnp.float32) * scale
    c_end = jnp.arange(n_cmp) * CMP_STRIDE + (CMP_BLOCK - 1)
    valid_c = c_end[None, :] <= t_pos[:, None]
    p_c = jax.nn.softmax(jnp.where(valid_c, s_c, NEG), axis=-1) * jnp.any(valid_c, axis=-1)[:, None]
    o_c = jnp.einsum('bgrtc,bcgd->btgrd', p_c.astype(v_c.dtype), v_c)
    n_sel = S // SEL_BLOCK
    c_start = np.arange(n_cmp) * CMP_STRIDE
    j_start = np.arange(n_sel) * SEL_BLOCK
    overlap = ((c_start[:, None] < j_start[None, :] + SEL_BLOCK)
               & (c_start[:, None] + CMP_BLOCK > j_start[None, :])).astype(np.float32)
    imp = jnp.einsum('bgrtc,cj->bgtj', p_c, jnp.asarray(overlap))
    j = jnp.arange(n_sel)[None, :]
    cur = (t_pos // SEL_BLOCK)[:, None]
    forced = (j == 0) | ((cur - j >= 0) & (cur - j < N_LOCAL_SEL))
    sel_score = jnp.where(forced, FORCE, jnp.where(j <= cur, imp, -FORCE))
    _, sel_idx = lax.top_k(sel_score, min(N_SEL, n_sel))
    o_s = _nsa_selected(q, sel_idx, k_s, v_s)
    o_w = _nsa_window(q, k_w, v_w)
    o = (gates[:, :, 0, :, :, None] * o_c + gates[:, :, 1, :, :, None] * o_s
         + gates[:, :, 2, :, :, None] * o_w).astype(hn.dtype)
    return o.reshape(B, S, NSA_HEADS * Dh) @ w_out


def setup_inputs(seed: int = 0) -> dict:
    key = jax.random.key(seed)
    ks = jax.random.split(key, 22)
    f32 = jnp.float32

    def dense(k, shape):
        return jax.random.normal(k, shape, f32) * shape[-2] ** -0.5

    def gain(k, shape):
        return 1.0 + 0.02 * jax.random.normal(k, shape, f32)

    H, G, Dh = NSA_HEADS, NSA_KV_HEADS, HEAD_DIM
    return {
        'x': jax.random.normal(ks[0], (BATCH, SEQ, D_MODEL), f32),
        'ffn1_norm': gain(ks[1], (DEPTH, D_MODEL)),
        'ffn1_w_in': dense(ks[2], (DEPTH, D_MODEL, 2 * D_FF)),
        'ffn1_w_out': dense(ks[3], (DEPTH, D_FF, D_MODEL)),
        'mix_norm': gain(ks[4], (DEPTH, D_MODEL)),
        'ffn2_norm': gain(ks[5], (DEPTH, D_MODEL)),
        'ffn2_w_in': dense(ks[6], (DEPTH, D_MODEL, 2 * D_FF)),
        'ffn2_w_out': dense(ks[7], (DEPTH, D_FF, D_MODEL)),
        'sb_w_qkv': dense(ks[8], (N_A_LAYERS, D_MODEL, 3 * SB_HEADS * Dh)),
        'sb_w_out': dense(ks[9], (N_A_LAYERS, SB_HEADS * Dh, D_MODEL)),
        'kv_norm': gain(ks[10], (D_MODEL,)),
        'nsa_w_kv': dense(ks[11], (D_MODEL, 2 * N_BRANCHES * G * Dh)),
        'nsa_k_norm': gain(ks[12], (N_BRANCHES, Dh)),
        'cmp_pos_k': 0.1 * jax.random.normal(ks[13], (CMP_BLOCK, Dh), f32),
        'cmp_pos_v': 0.1 * jax.random.normal(ks[14], (CMP_BLOCK, Dh), f32),
        'cmp_k_w1': dense(ks[15], (CMP_BLOCK * Dh, CMP_HIDDEN)),
        'cmp_k_w2': dense(ks[16], (CMP_HIDDEN, Dh)),
        'cmp_v_w1': dense(ks[17], (CMP_BLOCK * Dh, CMP_HIDDEN)),
        'cmp_v_w2': dense(ks[18], (CMP_HIDDEN, Dh)),
        'nsa_w_in': dense(ks[19], (N_B_LAYERS, D_MODEL, H * Dh + N_BRANCHES * H)),
        'nsa_q_norm': gain(ks[20], (N_B_LAYERS, Dh)),
        'nsa_w_out': dense(ks[21], (N_B_LAYERS, H * Dh, D_MODEL)),
    }


def reference(x, ffn1_norm, ffn1_w_in, ffn1_w_out, mix_norm, ffn2_norm, ffn2_w_in, ffn2_w_out,
              sb_w_qkv, sb_w_out, kv_norm, nsa_w_kv, nsa_k_norm, cmp_pos_k, cmp_pos_v,
              cmp_k_w1, cmp_k_w2, cmp_v_w1, cmp_v_w2, nsa_w_in, nsa_q_norm, nsa_w_out):
    B, S, _ = x.shape
    cos, sin = _rope_tables(jnp.arange(S))
    h = x
    kvs = None
    for layer in range(DEPTH):
        h = h + 0.5 * _swiglu(_rmsnorm(h, ffn1_norm[layer]), ffn1_w_in[layer], ffn1_w_out[layer])
        hn = _rmsnorm(h, mix_norm[layer])
        if layer < N_A_LAYERS:
            h = h + _stick_breaking_mixer(hn, sb_w_qkv[layer], sb_w_out[layer])
        else:
            i = layer - N_A_LAYERS
            h = h + _nsa_mixer(hn, kvs, nsa_w_in[i], nsa_q_norm[i], nsa_w_out[i], cos, sin)
        h = h + 0.5 * _swiglu(_rmsnorm(h, ffn2_norm[layer]), ffn2_w_in[layer], ffn2_w_out[layer])
        if layer == N_A_LAYERS - 1:
            kvs = _nsa_shared_kv(h, kv_norm, nsa_w_kv, nsa_k_norm, cmp_pos_k, cmp_pos_v,
                                 cmp_k_w1, cmp_k_w2, cmp_v_w1, cmp_v_w2, cos, sin)
    return h
```

```python
import os
from contextlib import ExitStack

import numpy as np
import concourse.bass as bass
import concourse.mybir as mybir
from concourse.bass_utils import run_bass_kernel_spmd

F32 = mybir.dt.float32
BF16 = mybir.dt.bfloat16
AF = mybir.ActivationFunctionType
ALU = mybir.AluOpType
AX = mybir.AxisListType

S = 2048
D = 1024
FF = 2816
NT = 16
NK = 8
NFC = 22
EPS = 1e-6
NEGM = -30000.0


class _Op:
    __slots__ = ("id", "eng", "fn", "deps", "dma", "sem", "val", "milestone", "is_out")


class Prog:
    ENGS = ("pe", "act", "dve", "pool", "sp")

    def __init__(self, nc, es, n_dma_sems=40):
        self.nc = nc
        self.ops = []
        self.stream = {e: [] for e in self.ENGS}
        self.lastw = {}
        self.readers = {}
        self.engsem = {e: es.enter_context(nc.semaphore("s_" + e)) for e in ("pe", "act", "dve", "pool")}
        self.dsems = [es.enter_context(nc.semaphore("d%d" % i)) for i in range(n_dma_sems)]
        self.dsem_last = [None] * n_dma_sems
        self.dsem_cnt = [0] * n_dma_sems
        self.dnext = 0
        self.pending_barrier = {}

    def _new(self, eng, fn, reads, writes, dma):
        op = _Op()
        op.id = len(self.ops)
        op.eng = eng
        op.fn = fn
        op.dma = dma
        op.milestone = False
        op.is_out = False
        op.sem = None
        op.val = 0
        deps = set()
        ops = self.ops
        for t in reads:
            w = self.lastw.get(t)
            if w is not None:
                if dma or ops[w].dma or ops[w].eng != eng or eng != "pe":
                    deps.add(w)
        for t in writes:
            w = self.lastw.get(t)
            if w is not None and (dma or ops[w].dma or ops[w].eng != eng):
                deps.add(w)
            rd = self.readers.get(t)
            if rd:
                for r in rd.values():
                    if isinstance(r, list):
                        deps.update(r)
                    elif dma or ops[r].eng != eng:
                        deps.add(r)
        pb = self.pending_barrier.pop(eng, None)
        if pb:
            deps.update(pb)
        for t in reads:
            rd = self.readers.setdefault(t, {})
            if dma:
                rd.setdefault("dma", []).append(op.id)
            else:
                rd[eng] = op.id
        for t in writes:
            self.lastw[t] = op.id
            self.readers[t] = {}
        op.deps = deps
        self.ops.append(op)
        self.stream[eng].append(op)
        return op

    def op(self, eng, fn, reads=(), writes=()):
        return self._new(eng, fn, reads, writes, False)

    def dma(self, eng, fn, reads=(), writes=(), is_out=False):
        op = self._new(eng, fn, reads, writes, True)
        k = self.dnext
        self.dnext = (k + 1) % len(self.dsems)
        prev = self.dsem_last[k]
        if prev is not None:
            op.deps.add(prev)
        self.dsem_cnt[k] += 1
        op.sem = self.dsems[k]
        op.val = 16 * self.dsem_cnt[k]
        self.dsem_last[k] = op.id
        op.is_out = is_out
        return op

    def barrier(self):
        b = set()
        for e in self.ENGS:
            for op in reversed(self.stream[e]):
                if not op.dma:
                    b.add(op.id)
                    break
        for k in self.dsem_last:
            if k is not None:
                b.add(k)
        for e in self.ENGS:
            self.pending_barrier[e] = set(b) | self.pending_barrier.get(e, set())

    def emit(self):
        ops = self.ops
        for op in ops:
            for d in op.deps:
                ops[d].milestone = True
        cnt = {e: 0 for e in self.ENGS}
        for op in ops:
            if not op.dma and op.milestone:
                cnt[op.eng] += 1
                op.val = cnt[op.eng]
                op.sem = self.engsem[op.eng]
        outs = [op for op in ops if op.dma and op.is_out]

        def body_for(eng):
            def body(e):
                waited = {}
                for op in self.stream[eng]:
                    need = {}
                    for d in op.deps:
                        dd = ops[d]
                        key = id(dd.sem)
                        if need.get(key, (None, 0))[1] < dd.val:
                            need[key] = (dd.sem, dd.val)
                    for key, (sem, v) in need.items():
                        if waited.get(key, 0) < v:
                            e.wait_ge(sem, v)
                            waited[key] = v
                    ins = op.fn(e)
                    if op.dma:
                        ins.then_inc(op.sem, 16)
                    elif op.milestone:
                        ins.then_inc(op.sem, 1)
                if eng == "sp":
                    for o in outs:
                        e.wait_ge(o.sem, o.val)
            return body

        with self.nc.Block() as block:
            block.tensor(body_for("pe"))
            block.scalar(body_for("act"))
            block.vector(body_for("dve"))
            block.gpsimd(body_for("pool"))
            block.sync(body_for("sp"))


def _consts():
    c = {}
    c["ident"] = np.eye(128, dtype=np.float32)
    sidx = np.arange(128)[:, None]
    tidx = np.arange(128)[None, :]
    c["trin"] = -(sidx >= tidx).astype(np.float32)
    c["onesn"] = -np.ones((128, 128), np.float32)
    c["sbneg"] = np.where(sidx >= tidx, NEGM, 0.0).astype(np.float32)
    c["sb01"] = (sidx < tidx).astype(np.float32)
    c["nsneg"] = np.where(sidx > tidx, NEGM, 0.0).astype(np.float32)
    c["farneg"] = np.where(sidx <= tidx, NEGM, 0.0).astype(np.float32)
    jj = np.arange(32)[:, None]
    ss_ = np.arange(S)[None, :]
    c["esel"] = np.where(ss_ // 64 == jj, NEGM, 0.0).astype(np.float32)
    cc = np.arange(128)[:, None]
    tt_ = np.arange(S)[None, :]
    c["cmask"] = np.where((16 * cc + 31 > tt_) | (cc == 127), NEGM, 0.0).astype(np.float32)
    t = np.arange(S)
    cur = (t // 64)[:, None]
    j = np.arange(32)[None, :]
    forced = (j == 0) | ((cur - j >= 0) & (cur - j < 2))
    m1 = ((~forced) & (j <= cur)).astype(np.float32)
    m2 = np.where(forced, 1e4, np.where(j <= cur, 0.0, -1e4)).astype(np.float32)
    c["m1"] = np.ascontiguousarray(m1.reshape(16, 128, 32).transpose(1, 0, 2))
    c["m2"] = np.ascontiguousarray(m2.reshape(16, 128, 32).transpose(1, 0, 2))
    cst = np.arange(128)[:, None] * 16
    jst = np.arange(32)[None, :] * 64
    c["ovl"] = ((cst < jst + 64) & (cst + 32 > jst)).astype(np.float32)
    c["ovl"][127, :] = 0.0
    inv = np.power(np.float32(500000.0), -np.arange(0, 16, 2, dtype=np.float32) / np.float32(16)).astype(np.float32)
    ang = t.astype(np.float32)[:, None] * inv[None, :]
    c["cos"] = np.ascontiguousarray(np.cos(ang).astype(np.float32).reshape(16, 128, 8).transpose(1, 0, 2))
    c["sin"] = np.ascontiguousarray(np.sin(ang).astype(np.float32).reshape(16, 128, 8).transpose(1, 0, 2))
    angc = (np.arange(128) * 16 + 31).astype(np.float32)[:, None] * inv[None, :]
    c["cosc"] = np.cos(angc).astype(np.float32)
    c["sinc"] = np.sin(angc).astype(np.float32)
    return c


class Builder:
    def __init__(self, stages):
        self.stages = stages
        self.nc = bass.Bass("TRN2", target_bir_lowering=False)
        self.es = ExitStack()

    def dram_in(self, name, shape):
        return self.nc.dram_tensor(name, list(shape), F32, kind="ExternalInput").ap()

    def sb(self, name, shape, dt):
        return self.es.enter_context(self.nc.sbuf_tensor(name, list(shape), dt))

    def build(self):
        nc, es = self.nc, self.es
        with es:
            self._build()
        return nc

    def carve(self, off_bytes, nbytes, dt, pattern=None, **kw):
        assert off_bytes % 4 == 0 and nbytes % 4 == 0
        assert off_bytes + nbytes <= self.arena_bytes, (off_bytes, nbytes)
        v = self.arena[:, off_bytes // 4:(off_bytes + nbytes) // 4]
        if dt != F32:
            v = v.bitcast(dt)
        if pattern:
            v = v.rearrange(pattern, **kw)
        return v

    def _build(self):
        nc = self.nc
        P = self.P = Prog(nc, self.es)
        di = self.dram_in
        self.x_d = di("x", (S, D))
        self.w = {}
        for name, shape in [
            ("ffn1_norm", (4, D)), ("ffn1_w_in", (4, D, 2 * FF)), ("ffn1_w_out", (4, FF, D)),
            ("mix_norm", (4, D)), ("ffn2_norm", (4, D)), ("ffn2_w_in", (4, D, 2 * FF)), ("ffn2_w_out", (4, FF, D)),
            ("sb_w_qkv", (2, D, 3 * D)), ("sb_w_out", (2, D, D)), ("kv_norm", (1, D)), ("nsa_w_kv", (D, 1536)),
            ("nsa_k_norm", (3, 64)), ("cmp_pos_k", (32, 64)), ("cmp_pos_v", (32, 64)),
            ("cmp_k_w1", (2048, 256)), ("cmp_k_w2", (256, 64)), ("cmp_v_w1", (2048, 256)), ("cmp_v_w2", (256, 64)),
            ("nsa_w_in", (2, D, 1072)), ("nsa_q_norm", (2, 64)), ("nsa_w_out", (2, D, D)),
        ]:
            self.w[name] = di(name, shape)
        self.c_d = {k: di("c_" + k, v.shape) for k, v in _consts().items()}
        self.out_d = nc.dram_tensor("out", [S, D], F32, kind="ExternalOutput").ap()

        self.h = self.sb("h", [128, NT, D], F32)
        self.xT = self.sb("xT", [128, NK, S], BF16)
        self.ident = self.sb("ident", [128, 128], BF16)
        self.ss = self.sb("ss", [128, NT], F32)
        self.rstd = self.sb("rstd", [128, NT], F32)
        self.epsc = self.sb("epsc", [128, 1], F32)
        self.arena_bytes = 110 * 1024
        self.arena = self.sb("arena", [128, self.arena_bytes // 4], F32)
        self.ps = [self.es.enter_context(nc.psum_tensor("ps%d" % i, [128, 512], F32)) for i in range(8)]
        self.norm_count = 0
        self.dbg_reg = {}
        self.nsa_stop = int(os.environ.get("MK_NSA_STOP", "0"))
        self.cb = {}
        for k in ("trin", "onesn", "sbneg", "sb01"):
            self.cb[k] = self.sb("sbc_" + k, [128, 128], BF16)
            P.dma("pool", lambda e, k=k: e.dma_start(out=self.cb[k][:], in_=self.c_d[k][:, :]), writes=["c_" + k])
        self.onec = self.sb("onec", [128, 1], F32)
        P.op("pool", lambda e: e.memset(self.onec[:], 1.0), writes=["onec"])

        P.dma("pool", lambda e: e.dma_start(out=self.ident[:], in_=self.c_d["ident"][:, :]), writes=["ident"])
        P.op("pool", lambda e: e.memset(self.epsc[:], EPS), writes=["epsc"])
        xv = self.x_d.rearrange("(t p) d -> p t d", p=128)
        for q in range(4):
            P.dma("sp", lambda e, q=q: e.dma_start(out=self.h[:, 4 * q:4 * q + 4, :], in_=xv[:, 4 * q:4 * q + 4, :]),
                  writes=[("h", t) for t in range(4 * q, 4 * q + 4)])

        st = self.stages
        n = 0
        prev = None
        for layer in range(4):
            for sub in ("a", "b", "c"):
                if n >= st:
                    break
                n += 1
                if sub in ("a", "c"):
                    pre = "ffn1" if sub == "a" else "ffn2"
                    w_in, w_out = self.w[pre + "_w_in"][layer], self.w[pre + "_w_out"][layer]
                    if prev != "ffn":
                        P.barrier()
                    self.ffn_prefetch(w_in, w_out)
                    self.norm_T(self.w[pre + "_norm"][layer:layer + 1, :])
                    self.ffn(w_in, w_out)
                    prev = "ffn"
                    if sub == "c" and layer == 1:
                        P.barrier()
                        self.nsa_kv()
                        prev = "mix"
                elif layer < 2:
                    P.barrier()
                    self.norm_T(self.w["mix_norm"][layer:layer + 1, :])
                    P.barrier()
                    self.sb_mixer(self.w["sb_w_qkv"][layer], self.w["sb_w_out"][layer])
                    prev = "mix"
                else:
                    i = layer - 2
                    P.barrier()
                    self.norm_T(self.w["mix_norm"][layer:layer + 1, :])
                    P.barrier()
                    self.nsa_mixer(self.w["nsa_w_in"][i], self.w["nsa_q_norm"][i:i + 1, :], self.w["nsa_w_out"][i])
                    prev = "mix"
        self.dbg_names = []
        if os.environ.get("MK_DBG"):
            kvv = dict(self.kv_views())
            kvv.update(self.dbg_reg)
            for nm in os.environ["MK_DBG"].split(","):
                ap = kvv[nm]
                shp = list(ap.shape)
                dd = nc.dram_tensor("dbg_" + nm, shp, F32, kind="ExternalOutput").ap()
                P.barrier()
                if len(shp) == 4:
                    for bi in range(shp[1]):
                        P.dma("pool", lambda e, dd=dd, ap=ap, bi=bi: e.dma_start(out=dd[:, bi], in_=ap[:, bi]), is_out=True)
                else:
                    P.dma("pool", lambda e, dd=dd, ap=ap: e.dma_start(out=dd, in_=ap), is_out=True)
                self.dbg_names.append("dbg_" + nm)
        ov = self.out_d.rearrange("(t p) d -> p t d", p=128)
        for q in range(4):
            P.dma("sp", lambda e, q=q: e.dma_start(out=ov[:, 4 * q:4 * q + 4, :], in_=self.h[:, 4 * q:4 * q + 4, :]),
                  reads=[("h", t) for t in range(4 * q, 4 * q + 4)], is_out=True)
        P.emit()

    def norm_T(self, g_row):
        P = self.P
        i = 0
        NB = 69632
        gbc = self.carve(NB, 4096, F32)
        self.hn = [self.carve(NB + 4096 + 2048 * b, 2048, BF16) for b in range(2)]
        self.junk = self.hn[0]
        P.dma("sp", lambda e: e.dma_start(out=gbc[:], in_=g_row.partition_broadcast(128)), writes=[("gbc", i)])
        for tt in range(NT):
            P.op("act", lambda e, tt=tt: e.activation(out=self.junk[:], in_=self.h[:, tt, :], func=AF.Square,
                                                      accum_out=self.ss[:, tt:tt + 1]),
                 reads=[("h", tt)], writes=[("hn", 0), ("hnB", 0), ("ss", tt)])
        P.op("act", lambda e: e.activation(out=self.rstd[:], in_=self.ss[:], func=AF.Ln, scale=1.0 / D,
                                           bias=self.epsc[:]),
             reads=[("ss", t) for t in range(NT)] + ["epsc"], writes=["rstd"])
        P.op("act", lambda e: e.activation(out=self.rstd[:], in_=self.rstd[:], func=AF.Exp, scale=-0.5),
             reads=["rstd"], writes=["rstd"])
        for tt in range(NT):
            hb = self.hn[tt % 2]
            CS = 768
            P.op("dve", lambda e, tt=tt, hb=hb: e.scalar_tensor_tensor(out=hb[:, 0:CS], in0=self.h[:, tt, 0:CS],
                                                                      scalar=self.rstd[:, tt:tt + 1], in1=gbc[:, 0:CS],
                                                                      op0=ALU.mult, op1=ALU.mult),
                 reads=[("h", tt), "rstd", ("gbc", i)], writes=[("hn", tt % 2)])
            P.op("pool", lambda e, tt=tt, hb=hb: e.tensor_tensor(out=hb[:, CS:D], in0=self.h[:, tt, CS:D], in1=gbc[:, CS:D],
                                                                op=ALU.mult),
                 reads=[("h", tt), ("gbc", i)], writes=[("hnB", tt % 2)])
            P.op("pool", lambda e, tt=tt, hb=hb: e.tensor_scalar(out=hb[:, CS:D], in0=hb[:, CS:D], scalar1=self.rstd[:, tt:tt + 1],
                                                                scalar2=None, op0=ALU.mult),
                 reads=[("hnB", tt % 2), "rstd"], writes=[("hnB", tt % 2)])
            bank = 6 + (tt % 2)
            pst = self.ps[bank][:, :].bitcast(BF16).rearrange("p (k t) -> p k t", k=NK)
            for kc in range(NK):
                P.op("pe", lambda e, kc=kc, hb=hb, pst=pst: e.transpose(pst[:, kc, :], hb[:, kc * 128:(kc + 1) * 128],
                                                                         self.ident[:]),
                     reads=[("hn", tt % 2) if kc < 6 else ("hnB", tt % 2), "ident"], writes=[("ps", bank)])
            P.op("act", lambda e, tt=tt, pst=pst: e.activation(out=self.xT[:, :, tt * 128:(tt + 1) * 128], in_=pst,
                                                              func=AF.Copy),
                 reads=[("ps", bank)], writes=[("xT", tt // 4)])

    def sb_mixer(self, w_qkv, w_out):
        P = self.P
        cv = self.carve
        qT = [cv(b * 4096, 4096, BF16) for b in range(2)]
        kT = [cv(8192 + b * 4096, 4096, BF16) for b in range(2)]
        vv = [cv(16384 + b * 8192, 8192, BF16, "p (b h n) -> p b h n", b=NT, h=2) for b in range(2)]
        wq = [cv(32768 + b * 6144, 6144, BF16, "p (k w n) -> p k w n", k=NK, w=3) for b in range(2)]
        oT = cv(45056, 32768, BF16, "p (c t) -> p c t", c=NK)
        ee = cv(77824, 2048, F32)
        spb = [cv(79872 + b * 1024, 1024, BF16) for b in range(2)]
        rs32 = cv(81920, 2048, F32)
        rsb = [cv(83968 + b * 1024, 1024, BF16) for b in range(2)]
        wb = [cv(86016 + b * 1024, 1024, BF16) for b in range(2)]
        wo2 = [cv(88064, 8192, BF16, "p (c n) -> p c n", c=4), cv(99328, 8192, BF16, "p (c n) -> p c n", c=4)]
        assert 99328 + 8192 <= self.arena_bytes
        ps = self.ps
        wq_v = w_qkv.rearrange("(k p) (w n) -> p k w n", p=128, w=3)
        wo_v = w_out.rearrange("(c p) n -> p c n", p=128)
        ident = self.ident

        for b in range(2):
            P.op("pool", lambda e, b=b: e.memset(vv[b][:, :, :, :], 0.0), writes=[("v", b)])

        def load_w(p):
            b = p % 2
            for w3 in range(3):
                P.dma("pool", lambda e, w3=w3: e.dma_start(out=wq[b][:, :, w3, :],
                                                           in_=wq_v[:, :, w3, p * 128:(p + 1) * 128]),
                      writes=[("wq", b)])

        def proj_pieces(p):
            b = p % 2
            pieces = []
            for which, dst, nm in ((0, qT, "qT"), (1, kT, "kT")):
                for tc in range(4):
                    def piece(which=which, dst=dst, nm=nm, tc=tc):
                        bank = 6 + (tc % 2)
                        for kc in range(NK):
                            P.op("pe", lambda e, kc=kc: e.matmul(
                                ps[bank][:, :], lhsT=wq[b][:, kc, which, :], rhs=self.xT[:, kc, tc * 512:(tc + 1) * 512],
                                start=(kc == 0), stop=(kc == NK - 1)),
                                reads=[("wq", b), ("xT", tc)], writes=[("ps", bank)])
                        if which == 0:
                            P.op("dve", lambda e: e.tensor_scalar(
                                out=dst[b][:, tc * 512:(tc + 1) * 512], in0=ps[bank][:, :], scalar1=0.125, scalar2=None,
                                op0=ALU.mult), reads=[("ps", bank)], writes=[(nm, b)])
                        else:
                            P.op("dve", lambda e: e.tensor_copy(out=dst[b][:, tc * 512:(tc + 1) * 512],
                                                                in_=ps[bank][:, :]),
                                 reads=[("ps", bank)], writes=[(nm, b)])
                    pieces.append(piece)
            for t4 in range(4):
                def piece(t4=t4):
                    bank = 6 + (t4 % 2)
                    for i in range(4):
                        tt = 4 * t4 + i
                        for kc in range(NK):
                            P.op("pe", lambda e, kc=kc, tt=tt, i=i: e.matmul(
                                ps[bank][:, i * 128:(i + 1) * 128], lhsT=self.xT[:, kc, tt * 128:(tt + 1) * 128],
                                rhs=wq[b][:, kc, 2, :], start=(kc == 0), stop=(kc == NK - 1)),
                                reads=[("wq", b), ("xT", tt // 4)], writes=[("ps", bank)])
                    pv = ps[bank][:, :].rearrange("p (i n) -> p i n", i=4)
                    for hh in range(2):
                        P.op("dve", lambda e, hh=hh: e.tensor_copy(
                            out=vv[b][:, 4 * t4:4 * t4 + 4, hh, 64 * hh:64 * hh + 64], in_=pv[:, :, 64 * hh:64 * hh + 64]),
                            reads=[("ps", bank)], writes=[("v", b)])
                pieces.append(piece)
            return pieces

        ee2 = [ee, cv(96256, 2048, F32)]
        sp3 = [spb[0], spb[1], cv(98304, 1024, BF16)]
        its = []
        for p in range(NK):
            for tq in range(4):
                for hh in range(2):
                    bmax = 4 * tq + 3
                    for blk in range(bmax, -1, -1):
                        its.append((p, tq, hh, blk))

        def geom(it):
            p, tq, hh, blk = it
            t0 = max(512 * tq, 128 * blk)
            N = 512 * (tq + 1) - t0
            c0 = t0 - 512 * tq
            return t0, N, c0, (blk >= 4 * tq), (blk == 4 * tq + 3)

        def slices(it):
            p, tq, hh, blk = it
            b = p % 2
            t0, N, c0, diag, first = geom(it)
            pr = slice(64 * hh, 64 * hh + 64)
            return kT[b][pr, blk * 128:(blk + 1) * 128], qT[b][pr, t0:t0 + N]

        HEAT = int(os.environ.get("MK_HEAT", "0"))
        NZ = 4

        def unit_of(n):
            n = min(max(n, 0), len(its) - 1)
            return 4 * its[n][0] + its[n][1]

        def heater(k):
            us = {unit_of(j) for j in range(k - 7, k + 1)}
            if len(us) != 1:
                return
            hb = 4 + ((us.pop() + 1) % 2)
            for _ in range(HEAT):
                P.op("pe", lambda e: e.matmul(ps[hb][:, :], lhsT=ident[:], rhs=self.xT[:, 0, 0:512], start=True, stop=True,
                                              skip_group_check=True), reads=["ident", ("xT", 0)], writes=[("ps", hb)])

        def st0(n):
            it = its[n]
            p = it[0]
            b = p % 2
            t0, N, c0, diag, first = geom(it)
            kslc, qslc = slices(it)
            zb = n % NZ
            P.op("pe", lambda e: e.matmul(ps[zb][:, 0:N], lhsT=kslc, rhs=qslc, start=True, stop=False, skip_group_check=True),
                 reads=[("kT", b), ("qT", b)], writes=[("ps", zb)])
            P.op("act", lambda e: e.activation(out=ee2[n % 2][:, 0:N], in_=ps[zb][:, 0:N], func=AF.Exp),
                 reads=[("ps", zb)], writes=[("ee", n % 2)])

        def st1(n):
            it = its[n]
            t0, N, c0, diag, first = geom(it)
            i3 = n % 3
            P.op("act", lambda e: e.activation(out=sp3[i3][:, 0:N], in_=ee2[n % 2][:, 0:N], func=AF.Ln, bias=self.onec[:]),
                 reads=[("ee", n % 2), "onec"], writes=[("sp", i3)])
            if diag:
                P.op("pool", lambda e: e.tensor_tensor(out=sp3[i3][:, 0:128], in0=sp3[i3][:, 0:128],
                                                       in1=self.cb["sb01"][:], op=ALU.mult),
                     reads=[("sp", i3), "c_sb01"], writes=[("sp", i3)])
            if first and c0 > 0:
                P.op("pool", lambda e: e.memset(rs32[:, 0:c0], 0.0), writes=["rs32"])

        def st2(n):
            it = its[n]
            p, tq, hh, blk = it
            b = p % 2
            t0, N, c0, diag, first = geom(it)
            kslc, qslc = slices(it)
            i3 = n % 3
            i2 = n % 2
            ab = n % NZ
            nacc = 2 + (0 if first else 1) + (1 if diag else 0)
            P.op("pe", lambda e: e.matmul(ps[ab][:, 0:N], lhsT=self.cb["trin"][:], rhs=sp3[i3][:, 0:N], start=False,
                                          stop=(nacc == 2), skip_group_check=True),
                 reads=["c_trin", ("sp", i3)], writes=[("ps", ab)])
            if not first:
                P.op("pe", lambda e: e.matmul(ps[ab][:, 0:N], lhsT=self.cb["onesn"][:], rhs=rsb[i2][:, c0:c0 + N], start=False,
                                              stop=(not diag), skip_group_check=True),
                     reads=["c_onesn", ("rsb", i2)], writes=[("ps", ab)])
            if diag:
                P.op("pe", lambda e: e.matmul(ps[ab][:, 0:128], lhsT=ident[:], rhs=self.cb["sbneg"][:], start=False, stop=True,
                                              skip_group_check=True), reads=["ident", "c_sbneg"], writes=[("ps", ab)])
            P.op("act", lambda e: e.activation(out=wb[i2][:, 0:N], in_=ps[ab][:, 0:N], func=AF.Exp),
                 reads=[("ps", ab)], writes=[("w", i2)])
            if blk > 0:
                if first:
                    P.op("dve", lambda e: e.tensor_copy(out=rs32[:, c0:c0 + N], in_=sp3[i3][:, 0:N]),
                         reads=[("sp", i3)], writes=["rs32"])
                else:
                    P.op("dve", lambda e: e.tensor_tensor(out=rs32[:, c0:c0 + N], in0=rs32[:, c0:c0 + N], in1=sp3[i3][:, 0:N],
                                                          op=ALU.add), reads=[("sp", i3), "rs32"], writes=["rs32"])
                c0n = max(512 * tq, 128 * (blk - 1)) - 512 * tq
                inx = (n + 1) % 2
                P.op("dve", lambda e: e.tensor_copy(out=rsb[inx][:, c0n:512], in_=rs32[:, c0n:512]),
                     reads=["rs32"], writes=[("rsb", inx)])

        def st3(n):
            it = its[n]
            p, tq, hh, blk = it
            b = p % 2
            t0, N, c0, diag, first = geom(it)
            i2 = n % 2
            ob = 4 + ((4 * p + tq) % 2)
            fm = (hh == 0 and first)
            lm = (hh == 1 and blk == 0)
            P.op("pe", lambda e: e.matmul(ps[ob][:, c0:c0 + N], lhsT=vv[b][:, blk, hh, :], rhs=wb[i2][:, 0:N], start=fm, stop=lm,
                                          skip_group_check=True), reads=[("v", b), ("w", i2)], writes=[("ps", ob)])
            if lm:
                P.op("dve", lambda e: e.tensor_copy(out=oT[:, p, tq * 512:(tq + 1) * 512], in_=ps[ob][:, :]),
                     reads=[("ps", ob)], writes=[("oT", p)])

        load_w(0)
        load_w(1)
        for half in range(2):
            P.dma("pool", lambda e, half=half: e.dma_start(out=wo2[half][:, :, :], in_=wo_v[:, 4 * half:4 * half + 4, :]),
                  writes=[("wo_sb", half)])
        for pc in proj_pieces(0):
            pc()
        nit = len(its)
        pieces = []
        cur_p = -1
        for k in range(nit + 3):
            if k < nit and its[k][0] != cur_p:
                cur_p = its[k][0]
                while pieces:
                    pieces.pop(0)()
                if cur_p + 1 < NK:
                    pieces = proj_pieces(cur_p + 1)
                if cur_p + 2 < NK:
                    load_w(cur_p + 2)
            if k < nit:
                st0(k)
            if 0 <= k - 1 < nit:
                st1(k - 1)
            if 0 <= k - 2 < nit:
                st2(k - 2)
            heater(k)
            if 0 <= k - 3 < nit:
                st3(k - 3)
            if pieces and k % 5 == 4:
                pieces.pop(0)()
        for half in range(2):
            wo = wo2[half]
            for tt in range(NT):
                for nh in range(2):
                    bank = 2 * (tt % 2) + nh
                    for c in range(4):
                        P.op("pe", lambda e, bank=bank, c=c, tt=tt, nh=nh, half=half: e.matmul(
                            ps[bank][:, :], lhsT=oT[:, 4 * half + c, tt * 128:(tt + 1) * 128],
                            rhs=wo2[half][:, c, nh * 512:(nh + 1) * 512], start=(c == 0), stop=(c == 3)),
                            reads=[("oT", 4 * half + c), ("wo_sb", half)], writes=[("ps", bank)])
                    P.op("dve", lambda e, bank=bank, tt=tt, nh=nh: e.tensor_tensor(
                        out=self.h[:, tt, nh * 512:(nh + 1) * 512], in0=ps[bank][:, :],
                        in1=self.h[:, tt, nh * 512:(nh + 1) * 512], op=ALU.add),
                        reads=[("ps", bank), ("h", tt)], writes=[("h", tt)])
        P.barrier()

    def head_norm_rope_gen(self, src, src_tok, H, gain, gain_tok, cos_t, sin_t, wk, tag, finish):
        P = self.P
        sq, qn, ssq, t1, t2, t3, t4 = wk["sq"], wk["qn"], wk["ssq"], wk["t1"], wk["t2"], wk["t3"], wk["t4"]
        n = H * 64
        T = lambda nm: (tag, nm)
        src3 = src.rearrange("p (h d) -> p h d", h=H)
        sq3 = sq[:, 0:n].rearrange("p (h d) -> p h d", h=H)
        qn3 = qn[:, 0:n].rearrange("p (h d) -> p h d", h=H)
        P.op("act", lambda e: e.activation(out=sq[:, 0:n], in_=src, func=AF.Square), reads=[src_tok], writes=[T("sq")])
        yield
        P.op("dve", lambda e: e.tensor_reduce(out=ssq[:, 0:H], in_=sq3, axis=AX.X, op=ALU.add),
             reads=[T("sq")], writes=[T("ssq")])
        yield
        P.op("act", lambda e: e.activation(out=ssq[:, 0:H], in_=ssq[:, 0:H], func=AF.Ln, scale=1.0 / 64,
                                           bias=self.epsc[:]), reads=[T("ssq"), "epsc"], writes=[T("ssq")])
        yield
        P.op("act", lambda e: e.activation(out=ssq[:, 0:H], in_=ssq[:, 0:H], func=AF.Exp, scale=-0.5),
             reads=[T("ssq")], writes=[T("ssq")])
        yield
        P.op("dve", lambda e: e.tensor_tensor(out=qn3, in0=src3, in1=ssq[:, 0:H].unsqueeze(2).to_broadcast([128, H, 64]),
                                              op=ALU.mult), reads=[src_tok, T("ssq")], writes=[T("qn")])
        yield
        P.op("dve", lambda e: e.tensor_tensor(out=qn3, in0=qn3, in1=gain.unsqueeze(1).to_broadcast([128, H, 64]),
                                              op=ALU.mult), reads=[T("qn"), gain_tok], writes=[T("qn")])
        yield
        x1 = qn3[:, :, 0:8]
        x2 = qn3[:, :, 8:16]
        cb = cos_t.unsqueeze(1).to_broadcast([128, H, 8])
        sb_ = sin_t.unsqueeze(1).to_broadcast([128, H, 8])
        tv = [t[:, 0:H * 8].rearrange("p (h d) -> p h d", h=H) for t in (t1, t2, t3, t4)]
        P.op("pool", lambda e: e.tensor_tensor(out=tv[0], in0=x1, in1=cb, op=ALU.mult), reads=[T("qn"), "rope"], writes=[T("t1")])
        P.op("pool", lambda e: e.tensor_tensor(out=tv[1], in0=x2, in1=sb_, op=ALU.mult), reads=[T("qn"), "rope"], writes=[T("t2")])
        P.op("dve", lambda e: e.tensor_tensor(out=tv[2], in0=x2, in1=cb, op=ALU.mult), reads=[T("qn"), "rope"], writes=[T("t3")])
        P.op("dve", lambda e: e.tensor_tensor(out=tv[3], in0=x1, in1=sb_, op=ALU.mult), reads=[T("qn"), "rope"], writes=[T("t4")])
        yield
        P.op("dve", lambda e: e.tensor_tensor(out=x1, in0=tv[0], in1=tv[1], op=ALU.subtract),
             reads=[T("t1"), T("t2"), T("t3"), T("t4")], writes=[T("qn")])
        P.op("dve", lambda e: e.tensor_tensor(out=x2, in0=tv[2], in1=tv[3], op=ALU.add),
             reads=[T("t3"), T("t4"), T("qn")], writes=[T("qn")])
        yield
        finish(qn3)

    @staticmethod
    def interleave(gens):
        gens = list(gens)
        while gens:
            for g_ in list(gens):
                try:
                    next(g_)
                except StopIteration:
                    gens.remove(g_)

    def head_norm_rope(self, src, src_tok, H, gain, gain_tok, cos_t, sin_t, wk, tag):
        res = []
        self.interleave([self.head_norm_rope_gen(src, src_tok, H, gain, gain_tok, cos_t, sin_t, wk, tag, res.append)])
        return res[0]

    def load_gain64(self, dst, row_ap, tok, scale=None):
        P = self.P
        P.dma("sp", lambda e: e.dma_start(out=dst, in_=row_ap.partition_broadcast(128)), writes=[tok])
        if scale is not None:
            P.op("pool", lambda e: e.tensor_scalar(out=dst, in0=dst, scalar1=float(scale), scalar2=None, op0=ALU.mult),
                 reads=[tok], writes=[tok])

    def kv_views(self):
        base = self.arena_bytes - 34816
        cv = self.carve
        v = {}
        v["base"] = base
        v["ksT"] = cv(base, 8192, BF16, "p (c t) -> p c t", c=2)
        v["kwT"] = cv(base + 8192, 8192, BF16, "p (c t) -> p c t", c=2)
        v["vs"] = cv(base + 16384, 8320, BF16, "p (b g n) -> p b g n", b=NT, g=4)
        v["vw"] = cv(base + 24704, 8320, BF16, "p (b g n) -> p b g n", b=NT, g=4)
        v["kcT"] = cv(base + 33024, 512, BF16, "p (c t) -> p c t", c=2)
        v["vc"] = cv(base + 33536, 776, BF16, "p (g n) -> p g n", g=4)
        return v

    def nsa_kv(self):
        P = self.P
        W = self.w
        self.norm_T(W["kv_norm"][0:1, :])
        kv = self.kv_views()
        cv = self.carve
        ps = self.ps
        ident = self.ident
        wkv = cv(0, 24576, BF16, "p (k n) -> p k n", k=NK)
        kcr = cv(24576, 8256, BF16, "p (c t) -> p c t", c=2)
        vcr = cv(32832, 8256, BF16, "p (c t) -> p c t", c=2)
        o = 41088
        wkA = []
        for _ in range(2):
            wk = {}
            for nm, nb in (("sq", 1024), ("qn", 1024), ("ssq", 64), ("t1", 128), ("t2", 128), ("t3", 128), ("t4", 128)):
                wk[nm] = cv(o, nb, F32)
                o += nb
            wkA.append(wk)
        gk = []
        for i in range(3):
            gk.append(cv(o, 256, F32)); o += 256
        cosb = cv(o, 512, F32, "p (t f) -> p t f", t=NT); o += 512
        sinb = cv(o, 512, F32, "p (t f) -> p t f", t=NT); o += 512
        cosc = cv(o, 32, F32); o += 32
        sinc = cv(o, 32, F32); o += 32
        kbf = [cv(o + b * 2048, 2048, BF16) for b in range(2)]
        o += 4096
        offA = o
        wkv_v = W["nsa_w_kv"].rearrange("(k p) n -> p k n", p=128)
        for kq in range(4):
            P.dma("pool", lambda e, kq=kq: e.dma_start(out=wkv[:, 2 * kq:2 * kq + 2, :], in_=wkv_v[:, 2 * kq:2 * kq + 2, :]),
                  writes=["wkv"])
        for i in range(3):
            self.load_gain64(gk[i], W["nsa_k_norm"][i:i + 1, :], ("gk", i))
        P.dma("sp", lambda e: e.dma_start(out=cosb, in_=self.c_d["cos"][:, :, :]), writes=["rope"])
        P.dma("sp", lambda e: e.dma_start(out=sinb, in_=self.c_d["sin"][:, :, :]), writes=["rope"])
        P.dma("sp", lambda e: e.dma_start(out=cosc, in_=self.c_d["cosc"][:, :]), writes=["rope"])
        P.dma("sp", lambda e: e.dma_start(out=sinc, in_=self.c_d["sinc"][:, :]), writes=["rope"])
        for t_, nm in ((kcr, "kcr"), (vcr, "vcr")):
            P.op("pool", lambda e, t_=t_: e.memset(t_[:, :, 2048:2064], 0.0), writes=[nm])
        for nm in ("vs", "vw"):
            P.op("pool", lambda e, nm=nm: e.memset(kv[nm][:, :, :, 64:65], 1.0), writes=[nm])
        P.op("pool", lambda e: e.memset(kv["vc"][:, :, 64:65], 1.0), writes=["vc"])
        for g in range(4):
            P.dma("pool", lambda e, g=g: e.dma_start(out=kv["vc"][:, g, 65:97], in_=self.c_d["ovl"][:, :]), writes=["vc"])

        pendA = []
        for tt in range(NT):
            par = tt % 2
            b0 = 4 * par
            for nb in range(3):
                for kc in range(NK):
                    P.op("pe", lambda e, nb=nb, kc=kc, tt=tt, b0=b0: e.matmul(
                        ps[b0 + nb][:, :], lhsT=self.xT[:, kc, tt * 128:(tt + 1) * 128], rhs=wkv[:, kc, nb * 512:(nb + 1) * 512],
                        start=(kc == 0), stop=(kc == NK - 1)), reads=[("xT", tt // 4), "wkv"], writes=[("ps", b0 + nb)])
            kb = kbf[par]
            while len(pendA) > 1:
                pendA.pop(0)()
            gens = []
            for (nb, gi, off, nm) in ((1, 1, 0, "ks"), (2, 2, 256, "kw")):
                def fin(qn3, off=off, kb=kb, nb=nb, par=par):
                    P.op("dve", lambda e: e.tensor_copy(out=kb[:, off:off + 256].rearrange("p (h d) -> p h d", h=4), in_=qn3),
                         reads=[("kvn%d" % nb, "qn")], writes=[("kbf", par)])
                gens.append(self.head_norm_rope_gen(ps[b0 + nb][:, 0:256], ("ps", b0 + nb), 4, gk[gi], ("gk", gi),
                                                    cosb[:, tt, :], sinb[:, tt, :], wkA[nb - 1], "kvn%d" % nb, fin))
            self.interleave(gens)
            P.op("dve", lambda e, kb=kb, b0=b0: e.tensor_copy(out=kb[:, 512:1024], in_=ps[b0][:, :]),
                 reads=[("ps", b0)], writes=[("kbf", par)])
            for (nb, nm) in ((1, "vs"), (2, "vw")):
                P.op("dve", lambda e, nb=nb, nm=nm, tt=tt, b0=b0: e.tensor_copy(
                    out=kv[nm][:, tt, :, 0:64], in_=ps[b0 + nb][:, 256:512].rearrange("p (g d) -> p g d", g=4)),
                    reads=[("ps", b0 + nb)], writes=[nm])
            def tail(tt=tt, b0=b0, par=par, kb=kb):
                tb = b0 + 3
                pst = ps[tb][:, :].bitcast(BF16).rearrange("p (k t) -> p k t", k=NK)
                for c in range(8):
                    P.op("pe", lambda e, c=c: e.transpose(pst[:, c, :], kb[:, c * 128:(c + 1) * 128], ident[:]),
                         reads=[("kbf", par), "ident"], writes=[("ps", tb)])
                tsl = slice(tt * 128, (tt + 1) * 128)
                for (c0, dst, nm) in ((0, kv["ksT"], "ksT"), (2, kv["kwT"], "kwT"), (4, kcr, "kcr"), (6, vcr, "vcr")):
                    P.op("act", lambda e, c0=c0, dst=dst: e.activation(
                        out=dst[:, :, tsl], in_=pst[:, c0:c0 + 2, :], func=AF.Copy), reads=[("ps", tb)], writes=[nm])
            pendA.append(tail)
        while pendA:
            pendA.pop(0)()
        P.barrier()
        w1d = cv(41088, 16384, BF16, "p (l n) -> p l n", l=32)
        o = 41088 + 16384
        w2b = cv(o, 256, BF16, "p (c n) -> p c n", c=2); o += 256
        posT = cv(o, 64, BF16); o += 64
        bias = cv(o, 8, F32); o += 8
        hT = cv(o, 512, BF16, "p (c n) -> p c n", c=2); o += 512
        wkB = {}
        for nm, nb in (("sq", 1024), ("qn", 1024), ("ssq", 64), ("t1", 128), ("t2", 128), ("t3", 128), ("t4", 128)):
            wkB[nm] = cv(o, nb, F32)
            o += nb
        gkc = cv(o, 256, F32); o += 256
        coscB = cv(o, 32, F32); o += 32
        sincB = cv(o, 32, F32); o += 32
        kcb = cv(o, 512, BF16); o += 512
        assert o <= kv["base"]
        self.load_gain64(gkc, W["nsa_k_norm"][0:1, :], "gkc")
        P.dma("sp", lambda e: e.dma_start(out=coscB, in_=self.c_d["cosc"][:, :]), writes=["ropeB"])
        P.dma("sp", lambda e: e.dma_start(out=sincB, in_=self.c_d["sinc"][:, :]), writes=["ropeB"])
        for X in range(2):
            w1 = W["cmp_k_w1" if X == 0 else "cmp_v_w1"].rearrange("(l d) n -> d l n", d=64)
            w2 = W["cmp_k_w2" if X == 0 else "cmp_v_w2"].rearrange("(c p) n -> p c n", p=128)
            pos = W["cmp_pos_k" if X == 0 else "cmp_pos_v"].rearrange("l d -> d l")
            raw = kcr if X == 0 else vcr
            rawtok = "kcr" if X == 0 else "vcr"
            for hf in range(2):
                for lh in range(2):
                    P.dma("pool", lambda e, hf=hf, lh=lh, w1=w1: e.dma_start(
                        out=w1d[64 * hf:64 * hf + 64, 16 * lh:16 * lh + 16, :], in_=w1[:, 16 * lh:16 * lh + 16, :]),
                        writes=["w1d"])
            P.dma("pool", lambda e, w2=w2: e.dma_start(out=w2b, in_=w2[:, :, :]), writes=["w2b"])
            P.dma("pool", lambda e, pos=pos: e.dma_start(out=posT[0:64, :], in_=pos, allow_slow_non_contiguous=True),
                  writes=["posT"])
            for ncn in range(2):
                for l in range(32):
                    P.op("pe", lambda e, ncn=ncn, l=l: e.matmul(
                        ps[7][:, ncn:ncn + 1], lhsT=w1d[0:64, l, ncn * 128:(ncn + 1) * 128], rhs=posT[0:64, l:l + 1],
                        start=(l == 0), stop=(l == 31)), reads=["w1d", "posT"], writes=[("ps", 7)])
            P.op("dve", lambda e: e.tensor_copy(out=bias, in_=ps[7][:, 0:2]), reads=[("ps", 7)], writes=["cbias"])
            rawv = raw.rearrange("p c (a s) -> p c a s", s=16)
            for g in range(4):
                pr = slice(64 * (g % 2), 64 * (g % 2) + 64)
                pair = g // 2
                for ncn in range(2):
                    bank = ncn
                    for l in range(32):
                        rhs = rawv[pr, pair, 0:128, l] if l < 16 else rawv[pr, pair, 1:129, l - 16]
                        P.op("pe", lambda e, bank=bank, l=l, ncn=ncn, rhs=rhs, pr=pr: e.matmul(
                            ps[bank][:, 0:128], lhsT=w1d[pr, l, ncn * 128:(ncn + 1) * 128], rhs=rhs,
                            start=(l == 0), stop=(l == 31)), reads=["w1d", rawtok], writes=[("ps", bank)])
                    P.op("act", lambda e, bank=bank, ncn=ncn: e.activation(
                        out=hT[:, ncn, :], in_=ps[bank][:, 0:128], func=AF.Silu, bias=bias[:, ncn:ncn + 1]),
                        reads=[("ps", bank), "cbias"], writes=["hT"])
                for ncn in range(2):
                    P.op("pe", lambda e, ncn=ncn, g=g: e.matmul(
                        ps[2][:, g * 64:(g + 1) * 64], lhsT=hT[:, ncn, :], rhs=w2b[:, ncn, :],
                        start=(ncn == 0), stop=(ncn == 1)), reads=["hT", "w2b"], writes=[("ps", 2)])
            if X == 0:
                qn3 = self.head_norm_rope(ps[2][:, 0:256], ("ps", 2), 4, gkc, "gkc", coscB, sincB, wkB, "kcn")
                P.op("dve", lambda e, qn3=qn3: e.tensor_copy(out=kcb.rearrange("p (h d) -> p h d", h=4), in_=qn3),
                     reads=[("kcn", "qn")], writes=["kcb"])
                pst = ps[3][:, :].bitcast(BF16).rearrange("p (k t) -> p k t", k=NK)
                for c in range(2):
                    P.op("pe", lambda e, c=c, pst=pst: e.transpose(pst[:, c, :], kcb[:, c * 128:(c + 1) * 128], ident[:]),
                         reads=["kcb", "ident"], writes=[("ps", 3)])
                P.op("act", lambda e, pst=pst: e.activation(out=kv["kcT"][:, :, :], in_=pst[:, 0:2, :], func=AF.Copy),
                     reads=[("ps", 3)], writes=["kcT"])
            else:
                P.op("dve", lambda e: e.tensor_copy(out=kv["vc"][:, :, 0:64],
                                                    in_=ps[2][:, 0:256].rearrange("p (g d) -> p g d", g=4)),
                     reads=[("ps", 2)], writes=["vc"])
        P.barrier()

    def nsa_mixer(self, w_in, q_norm_row, w_out):
        P = self.P
        kv = self.kv_views()
        cv = self.carve
        ps = self.ps
        ident = self.ident
        qT = cv(0, 32768, BF16, "p (c t) -> p c t", c=NK)
        gates = cv(32768, 3072, F32, "p (t n) -> p t n", t=NT)
        cosb = cv(35840, 512, F32, "p (t f) -> p t f", t=NT)
        sinb = cv(36352, 512, F32, "p (t f) -> p t f", t=NT)
        o0 = 36864
        win = cv(o0, 17152, BF16, "p (k n) -> p k n", k=NK)
        o = o0 + 17152
        wk2 = []
        for _ in range(2):
            wk = {}
            for nm, nb in (("sq", 2048), ("qn", 2048), ("ssq", 64), ("t1", 256), ("t2", 256), ("t3", 256), ("t4", 256)):
                wk[nm] = cv(o, nb, F32)
                o += nb
            wk2.append(wk)
        gq = cv(o, 256, F32); o += 256
        eg = cv(o, 192, F32); o += 192
        qbf = [cv(o + b * 2048, 2048, BF16) for b in range(2)]
        o += 4096
        assert o <= kv["base"], o
        win_v = w_in.rearrange("(k p) n -> p k n", p=128)
        for kq in range(4):
            P.dma("pool", lambda e, kq=kq: e.dma_start(out=win[:, 2 * kq:2 * kq + 2, :], in_=win_v[:, 2 * kq:2 * kq + 2, :]),
                  writes=["win"])
        self.load_gain64(gq, q_norm_row, "gq", scale=0.125)
        P.dma("sp", lambda e: e.dma_start(out=cosb, in_=self.c_d["cos"][:, :, :]), writes=["rope"])
        P.dma("sp", lambda e: e.dma_start(out=sinb, in_=self.c_d["sin"][:, :, :]), writes=["rope"])
        pendQ = []
        for tt in range(NT):
            par = tt % 2
            b0 = 4 * par
            for nb, (c0, c1) in enumerate(((0, 512), (512, 1024), (1024, 1072))):
                for kc in range(NK):
                    P.op("pe", lambda e, nb=nb, kc=kc, c0=c0, c1=c1, tt=tt, b0=b0: e.matmul(
                        ps[b0 + nb][:, 0:c1 - c0], lhsT=self.xT[:, kc, tt * 128:(tt + 1) * 128], rhs=win[:, kc, c0:c1],
                        start=(kc == 0), stop=(kc == NK - 1)), reads=[("xT", tt // 4), "win"], writes=[("ps", b0 + nb)])
            qb = qbf[par]
            while len(pendQ) > 1:
                pendQ.pop(0)()
            qbv = qb.rearrange("p (hs r pr d) -> p hs pr r d", hs=2, r=4, pr=2)
            gens = []
            for hs in range(2):
                def fin(qn3, hs=hs, qbv=qbv, par=par):
                    P.op("dve", lambda e: e.tensor_copy(out=qbv[:, hs], in_=qn3.rearrange("p (pr r) d -> p pr r d", pr=2)),
                         reads=[("qn%d" % hs, "qn")], writes=[("qbf", par)])
                gens.append(self.head_norm_rope_gen(ps[b0 + hs][:, :], ("ps", b0 + hs), 8, gq, "gq", cosb[:, tt, :],
                                                    sinb[:, tt, :], wk2[hs], "qn%d" % hs, fin))
            self.interleave(gens)
            P.op("act", lambda e, b0=b0: e.activation(out=eg[:, 0:48], in_=ps[b0 + 2][:, 0:48], func=AF.Exp, scale=-1.0),
                 reads=[("ps", b0 + 2)], writes=["eg"])
            P.op("dve", lambda e: e.tensor_scalar(out=eg[:, 0:48], in0=eg[:, 0:48], scalar1=1.0, scalar2=None, op0=ALU.add),
                 reads=["eg"], writes=["eg"])
            P.op("dve", lambda e, tt=tt: e.reciprocal(out=gates[:, tt, :], in_=eg[:, 0:48]), reads=["eg"], writes=["gates"])
            def tail(tt=tt, b0=b0, par=par, qb=qb):
                tb = b0 + 3
                pst = ps[tb][:, :].bitcast(BF16).rearrange("p (k t) -> p k t", k=NK)
                for c in range(NK):
                    P.op("pe", lambda e, c=c: e.transpose(pst[:, c, :], qb[:, c * 128:(c + 1) * 128], ident[:]),
                         reads=[("qbf", par), "ident"], writes=[("ps", tb)])
                P.op("act", lambda e: e.activation(out=qT[:, :, tt * 128:(tt + 1) * 128], in_=pst, func=AF.Copy),
                     reads=[("ps", tb)], writes=["qT"])
            pendQ.append(tail)
        while pendQ:
            pendQ.pop(0)()
        P.barrier()
        self.dbg_reg["qT"] = qT
        self.dbg_reg["gates"] = gates
        if self.nsa_stop == 1:
            return
        o = o0
        esel = cv(o, 4096, BF16); o += 4096
        cmask = cv(o, 4096, BF16); o += 4096
        m1 = cv(o, 1024, BF16, "p (t j) -> p t j", t=NT); o += 1024
        m2 = cv(o, 1024, BF16, "p (t j) -> p t j", t=NT); o += 1024
        nsneg = cv(o, 256, BF16); o += 256
        farneg = cv(o, 256, BF16); o += 256
        PT = [cv(o + i * 1024, 1024, BF16) for i in range(12)]; o += 12288
        nselT = cv(o, 1024, BF16); o += 1024
        oacc = cv(o, 4096, F32, "p (i n) -> p i n", i=4); o += 4096
        obf = cv(o, 8192, BF16, "p (i n) -> p i n", i=4); o += 8192
        den = rec = scl = imp = sc2 = m8 = nsel = None
        oT = self.xT
        for (dst, nm, rows) in ((esel, "esel", 32), (cmask, "cmask", 128), (nsneg, "nsneg", 128), (farneg, "farneg", 128)):
            P.dma("pool", lambda e, dst=dst, nm=nm, rows=rows: e.dma_start(out=dst[0:rows, :], in_=self.c_d[nm][:, :]),
                  writes=["c_" + nm])
        P.dma("pool", lambda e: e.dma_start(out=m1, in_=self.c_d["m1"][:, :, :]), writes=["m1"])
        P.dma("pool", lambda e: e.dma_start(out=m2, in_=self.c_d["m2"][:, :, :]), writes=["m2"])
        SB_ = (0, 1, 2, 3)
        OB = (4, 5, 6, 7)
        cnt = [0]

        def nxt():
            i = cnt[0] % 8
            cnt[0] += 1
            return i

        brs = [int(c) for c in os.environ.get("MK_NSA_BR", "012")]

        def evac(tq, g, br, width, first_branch):
            first_branch = (br == brs[0])
            for i in range(4):
                T_ = 4 * tq + i
                Ov = ps[OB[i]][:, 0:4 * width].rearrange("p (r n) -> p r n", r=4)
                P.op("dve", lambda e, Ov=Ov: e.tensor_scalar(out=den[:, 0:4], in0=Ov[:, :, 64], scalar1=1e-30, scalar2=None,
                                                            op0=ALU.max), reads=[("ps", OB[i])], writes=["den"])
                P.op("dve", lambda e: e.reciprocal(out=rec[:, 0:4], in_=den[:, 0:4]), reads=["den"], writes=["rec"])
                P.op("dve", lambda e, T_=T_: e.tensor_tensor(out=scl[:, 0:4], in0=rec[:, 0:4],
                                                             in1=gates[:, T_, br * 16 + 4 * g:br * 16 + 4 * g + 4], op=ALU.mult),
                     reads=["rec", "gates"], writes=["scl"])
                for r in range(4):
                    if br not in brs:
                        continue
                    if first_branch:
                        P.op("dve", lambda e, Ov=Ov, r=r, i=i: e.tensor_scalar(
                            out=oacc[:, i, r * 64:(r + 1) * 64], in0=Ov[:, r, 0:64], scalar1=scl[:, r:r + 1], scalar2=None,
                            op0=ALU.mult), reads=[("ps", OB[i]), "scl"], writes=[("oacc", i)])
                    else:
                        P.op("dve", lambda e, Ov=Ov, r=r, i=i: e.scalar_tensor_tensor(
                            out=oacc[:, i, r * 64:(r + 1) * 64], in0=Ov[:, r, 0:64], scalar=scl[:, r:r + 1],
                            in1=oacc[:, i, r * 64:(r + 1) * 64], op0=ALU.mult, op1=ALU.add),
                            reads=[("ps", OB[i]), "scl", ("oacc", i)], writes=[("oacc", i)])
                if br == 0:
                    for r in range(4):
                        if r == 0:
                            P.op("dve", lambda e, Ov=Ov: e.tensor_scalar(out=imp[:, 0:32], in0=Ov[:, 0, 65:97],
                                                                        scalar1=rec[:, 0:1], scalar2=None, op0=ALU.mult),
                                 reads=[("ps", OB[i]), "rec"], writes=["imp"])
                        else:
                            P.op("dve", lambda e, Ov=Ov, r=r: e.scalar_tensor_tensor(
                                out=imp[:, 0:32], in0=Ov[:, r, 65:97], scalar=rec[:, r:r + 1], in1=imp[:, 0:32],
                                op0=ALU.mult, op1=ALU.add), reads=[("ps", OB[i]), "rec", "imp"], writes=["imp"])
                    P.op("dve", lambda e, T_=T_: e.tensor_tensor(out=sc2[:, 0:32], in0=imp[:, 0:32], in1=m1[:, T_, :], op=ALU.mult),
                         reads=["imp", "m1"], writes=["sc2"])
                    P.op("dve", lambda e, T_=T_: e.tensor_tensor(out=sc2[:, 0:32], in0=sc2[:, 0:32], in1=m2[:, T_, :], op=ALU.add),
                         reads=["sc2", "m2"], writes=["sc2"])
                    P.op("dve", lambda e: e.max(out=m8[:, 0:8], in_=sc2[:, 0:32]), reads=["sc2"], writes=["m8"])
                    P.op("dve", lambda e: e.tensor_scalar(out=nsel[:, 0:32], in0=sc2[:, 0:32], scalar1=m8[:, 7:8], scalar2=None,
                                                          op0=ALU.is_lt), reads=["sc2", "m8"], writes=["nsel"])
                    sbk = SB_[i]
                    pst = ps[sbk][:, :].bitcast(BF16)
                    P.op("pe", lambda e, pst=pst: e.transpose(pst[0:32, 0:128], nsel[:, 0:32], ident[:]),
                         reads=["nsel", "ident"], writes=[("ps", sbk)])
                    P.op("act", lambda e, pst=pst, i=i: e.activation(out=nselT[0:32, i * 128:(i + 1) * 128], in_=pst[0:32, 0:128],
                                                                    func=AF.Copy), reads=[("ps", sbk)], writes=["nselT"])

        tmp4s = [cv(o + 1024 * b_, 1024, F32, "p (r n) -> p r n", r=4) for b_ in range(2)]; o += 2048
        den4 = cv(o, 64, F32); o += 64
        rec4 = cv(o, 64, F32); o += 64
        scl4 = cv(o, 64, F32); o += 64
        imp4 = cv(o, 512, F32); o += 512
        sc4 = cv(o, 512, F32); o += 512
        m84 = cv(o, 128, F32); o += 128
        nsel4 = cv(o, 256, BF16); o += 256
        assert o <= kv["base"], o

        def evac2(tq, g, br, width):
            Ovs = [ps[OB[i]][:, 0:4 * width].rearrange("p (r n) -> p r n", r=4) for i in range(4)]
            R4 = lambda t_, i: t_[:, 4 * i:4 * i + 4]
            for i in range(4):
                if br == 0:
                    P.op("dve", lambda e, i=i: e.tensor_scalar(out=R4(den4, i), in0=Ovs[i][:, :, 64], scalar1=1e-30, scalar2=None,
                                                              op0=ALU.max), reads=[("ps", OB[i])], writes=[("den", i)])
                else:
                    P.op("dve", lambda e, i=i: e.reciprocal(out=R4(rec4, i), in_=Ovs[i][:, :, 64]), reads=[("ps", OB[i])],
                         writes=[("rec", i)])
            if br == 0:
                for i in range(4):
                    P.op("dve", lambda e, i=i: e.reciprocal(out=R4(rec4, i), in_=R4(den4, i)), reads=[("den", i)], writes=[("rec", i)])
            for i in range(4):
                T_ = 4 * tq + i
                P.op("dve", lambda e, i=i, T_=T_: e.tensor_tensor(out=R4(scl4, i), in0=R4(rec4, i),
                                                                  in1=gates[:, T_, br * 16 + 4 * g:br * 16 + 4 * g + 4], op=ALU.mult),
                     reads=[("rec", i), "gates"], writes=[("scl", i)])
            for i in range(4):
                oav = oacc[:, i, :].rearrange("p (r d) -> p r d", r=4)
                sclb = R4(scl4, i).unsqueeze(2).to_broadcast([128, 4, 64])
                if br == 0:
                    P.op("dve", lambda e, i=i, oav=oav, sclb=sclb: e.tensor_tensor(out=oav, in0=Ovs[i][:, :, 0:64], in1=sclb, op=ALU.mult),
                         reads=[("ps", OB[i]), ("scl", i)], writes=[("oacc", i)])
                else:
                    t4 = tmp4s[i % 2][:, :, 0:64]
                    P.op("dve", lambda e, i=i, sclb=sclb, t4=t4: e.tensor_tensor(out=t4, in0=Ovs[i][:, :, 0:64], in1=sclb, op=ALU.mult),
                         reads=[("ps", OB[i]), ("scl", i)], writes=[("tmp4", i % 2)])
                    P.op("pool", lambda e, oav=oav, t4=t4: e.tensor_tensor(out=oav, in0=oav, in1=t4, op=ALU.add),
                         reads=[("tmp4", i % 2), ("oacc", i)], writes=[("oacc", i)])
            if br == 0:
                I32 = lambda t_, i: t_[:, 32 * i:32 * i + 32]
                for i in range(4):
                    recb = R4(rec4, i).unsqueeze(2).to_broadcast([128, 4, 32])
                    t4 = tmp4s[i % 2][:, :, 0:32]
                    P.op("dve", lambda e, i=i, recb=recb, t4=t4: e.tensor_tensor(out=t4, in0=Ovs[i][:, :, 65:97], in1=recb, op=ALU.mult),
                         reads=[("ps", OB[i]), ("rec", i)], writes=[("tmp4", i % 2)])
                    P.op("dve", lambda e, i=i, t4=t4: e.tensor_reduce(out=I32(imp4, i), in_=t4.rearrange("p r j -> p j r"), axis=AX.X,
                                                                      op=ALU.add), reads=[("tmp4", i % 2)], writes=[("imp", i)])
                for i in range(4):
                    T_ = 4 * tq + i
                    P.op("dve", lambda e, i=i, T_=T_: e.tensor_tensor(out=I32(sc4, i), in0=I32(imp4, i), in1=m1[:, T_, :], op=ALU.mult),
                         reads=[("imp", i), "m1"], writes=[("sc2", i)])
                for i in range(4):
                    T_ = 4 * tq + i
                    P.op("dve", lambda e, i=i, T_=T_: e.tensor_tensor(out=I32(sc4, i), in0=I32(sc4, i), in1=m2[:, T_, :], op=ALU.add),
                         reads=[("sc2", i), "m2"], writes=[("sc2", i)])
                for i in range(4):
                    P.op("dve", lambda e, i=i: e.max(out=m84[:, 8 * i:8 * i + 8], in_=I32(sc4, i)), reads=[("sc2", i)], writes=[("m8", i)])
                for i in range(4):
                    P.op("dve", lambda e, i=i: e.tensor_scalar(out=I32(nsel4, i), in0=I32(sc4, i), scalar1=m84[:, 8 * i + 7:8 * i + 8],
                                                              scalar2=None, op0=ALU.is_lt), reads=[("sc2", i), ("m8", i)], writes=[("nsel", i)])

        def do_group(tq, g, early_hook=None):
            half = slice(64 * (g % 2), 64 * (g % 2) + 64)
            kp = g // 2
            its = [(0, 0)]
            for blk in range(max(0, 4 * tq - 4), 4 * tq + 4):
                its.append((2, blk))
            for blk in range(4 * tq + 4):
                its.append((1, blk))
            started = {0: [False] * 4, 1: [False] * 4, 2: [False] * 4}
            base = cnt[0]
            cnt[0] += len(its)

            def stA(k):
                for ph in range(5):
                    for r in range(4):
                        stA_r(k, r, ph)

            def stB(k):
                for r in range(4):
                    stB_r(k, r)

            def stA_r(k, r, ph):
                br, blk = its[k]
                n = base + k
                sbk = SB_[r]
                pi = 4 * (n % 3) + r
                slot = 4 * kp + r
                if br == 0:
                    N = 512
                    if ph == 0:
                        P.op("pe", lambda e: e.matmul(ps[sbk][:, :], lhsT=kv["kcT"][half, kp, :], rhs=qT[half, slot, tq * 512:(tq + 1) * 512],
                                                      start=True, stop=False, skip_group_check=True), reads=["kcT", "qT"], writes=[("ps", sbk)])
                    elif ph == 1:
                        P.op("pe", lambda e: e.matmul(ps[sbk][:, :], lhsT=ident[:], rhs=cmask[:, tq * 512:(tq + 1) * 512], start=False, stop=True,
                                                      skip_group_check=True), reads=["ident", "c_cmask"], writes=[("ps", sbk)])
                elif br == 1:
                    t0 = max(512 * tq, 128 * blk)
                    N = 512 * (tq + 1) - t0
                    c0 = t0 - 512 * tq
                    diag = blk >= 4 * tq
                    use_sel = tq > 0
                    if ph == 0:
                        P.op("pe", lambda e: e.matmul(ps[sbk][:, 0:N], lhsT=kv["ksT"][half, kp, blk * 128:(blk + 1) * 128],
                                                      rhs=qT[half, slot, t0:t0 + N], start=True, stop=(not use_sel and not diag),
                                                      skip_group_check=True),
                             reads=["ksT", "qT"], writes=[("ps", sbk)])
                    elif ph == 1 and use_sel:
                        P.op("pe", lambda e: e.matmul(ps[sbk][:, 0:N], lhsT=esel[0:32, blk * 128:(blk + 1) * 128], rhs=nselT[0:32, c0:c0 + N],
                                                      start=False, stop=(not diag), skip_group_check=True),
                             reads=["c_esel", "nselT"], writes=[("ps", sbk)])
                    elif ph == 2 and diag:
                        P.op("pe", lambda e: e.matmul(ps[sbk][:, 0:128], lhsT=ident[:], rhs=nsneg[:], start=False, stop=True,
                                                      skip_group_check=True), reads=["ident", "c_nsneg"], writes=[("ps", sbk)])
                else:
                    Tlo = max(blk, 4 * tq)
                    Thi = min(blk + 4, 4 * tq + 3)
                    N = 128 * (Thi - Tlo + 1)
                    t0 = 128 * Tlo
                    dmask = (Tlo == blk)
                    fmask = (Thi == blk + 4)
                    if ph == 0:
                        P.op("pe", lambda e: e.matmul(ps[sbk][:, 0:N], lhsT=kv["kwT"][half, kp, blk * 128:(blk + 1) * 128],
                                                      rhs=qT[half, slot, t0:t0 + N], start=True, stop=(not dmask and not fmask),
                                                      skip_group_check=True), reads=["kwT", "qT"], writes=[("ps", sbk)])
                    elif ph == 2 and dmask:
                        P.op("pe", lambda e: e.matmul(ps[sbk][:, 0:128], lhsT=ident[:], rhs=nsneg[:], start=False, stop=(not fmask),
                                                      skip_group_check=True), reads=["ident", "c_nsneg"], writes=[("ps", sbk)])
                    elif ph == 3 and fmask:
                        P.op("pe", lambda e: e.matmul(ps[sbk][:, N - 128:N], lhsT=ident[:], rhs=farneg[:], start=False, stop=True,
                                                      skip_group_check=True), reads=["ident", "c_farneg"], writes=[("ps", sbk)])
                if ph == 4:
                    P.op("act", lambda e: e.activation(out=PT[pi][:, 0:N], in_=ps[sbk][:, 0:N], func=AF.Exp),
                         reads=[("ps", sbk)], writes=[("PT", pi)])

            def stB_r(k, r):
                br, blk = its[k]
                n = base + k
                pi = 4 * (n % 3) + r
                st_ = started[br]
                if br == 0:
                    for i in range(4):
                        P.op("pe", lambda e, i=i, s0=(not st_[i]): e.matmul(
                            ps[OB[i]][:, r * 97:(r + 1) * 97], lhsT=PT[pi][:, i * 128:(i + 1) * 128], rhs=kv["vc"][:, g, :],
                            start=s0, stop=(r == 3), skip_group_check=True), reads=[("PT", pi), "vc"], writes=[("ps", OB[i])])
                        st_[i] = True
                    if r == 3:
                        evac2(tq, g, 0, 97)
                elif br == 1:
                    t0 = max(512 * tq, 128 * blk)
                    c0 = t0 - 512 * tq
                    nblk = 4 * tq + 4
                    for i in range(c0 // 128, 4):
                        lo = 128 * i - c0
                        lastw = (r == 3) and (blk == min(4 * tq + i, nblk - 1))
                        P.op("pe", lambda e, i=i, lo=lo, s0=(not st_[i]), lastw=lastw: e.matmul(
                            ps[OB[i]][:, r * 65:(r + 1) * 65], lhsT=PT[pi][:, lo:lo + 128], rhs=kv["vs"][:, blk, g, :],
                            start=s0, stop=lastw, skip_group_check=True), reads=[("PT", pi), "vs"], writes=[("ps", OB[i])])
                        st_[i] = True
                    if blk == nblk - 1 and r == 3:
                        evac2(tq, g, 1, 65)
                        for i in range(4):
                            P.op("pool", lambda e, i=i: e.tensor_copy(out=obf[:, i, g * 256:(g + 1) * 256], in_=oacc[:, i, :]),
                                 reads=[("oacc", i)], writes=[("obf", i)])
                else:
                    Tlo = max(blk, 4 * tq)
                    Thi = min(blk + 4, 4 * tq + 3)
                    i0, i1 = Tlo - 4 * tq, Thi - 4 * tq
                    for i in range(i0, i1 + 1):
                        lo = 128 * (i - i0)
                        lastw = (r == 3) and (blk == 4 * tq + i)
                        P.op("pe", lambda e, i=i, lo=lo, s0=(not st_[i]), lastw=lastw: e.matmul(
                            ps[OB[i]][:, r * 65:(r + 1) * 65], lhsT=PT[pi][:, lo:lo + 128], rhs=kv["vw"][:, blk, g, :],
                            start=s0, stop=lastw, skip_group_check=True), reads=[("PT", pi), "vw"], writes=[("ps", OB[i])])
                        st_[i] = True
                    if blk == 4 * tq + 3 and r == 3:
                        evac2(tq, g, 2, 65)

            def sel_transposes():
                sb0 = SB_[0]
                pst = ps[sb0][:, :].bitcast(BF16)
                for i in range(4):
                    P.op("pe", lambda e, i=i: e.transpose(pst[0:32, i * 128:(i + 1) * 128], nsel4[:, 32 * i:32 * i + 32], ident[:]),
                         reads=[("nsel", i), "ident"], writes=[("ps", sb0)])
                P.op("act", lambda e: e.activation(out=nselT[0:32, 0:512], in_=pst[0:32, 0:512], func=AF.Copy),
                     reads=[("ps", sb0)], writes=["nselT"])

            nit = len(its)
            SK = 2
            first_sel = next(i_ for i_, it_ in enumerate(its) if it_[0] == 1)
            for k in range(nit + SK):
                if k == first_sel and tq > 0:
                    sel_transposes()
                if k < nit:
                    stA(k)
                if k - SK >= 0:
                    stB(k - SK)
                if k == 3 and early_hook is not None:
                    early_hook()

        def o_transposes(tq):
            for i in range(4):
                T_ = 4 * tq + i
                sbk = SB_[i]
                pst = ps[sbk][:, :].bitcast(BF16).rearrange("p (k t) -> p k t", k=NK)
                for c in range(NK):
                    P.op("pe", lambda e, c=c, i=i, pst=pst: e.transpose(pst[:, c, :], obf[:, i, c * 128:(c + 1) * 128], ident[:]),
                         reads=[("obf", i), "ident"], writes=[("ps", sbk)])
                P.op("act", lambda e, T_=T_, pst=pst: e.activation(out=oT[:, :, T_ * 128:(T_ + 1) * 128], in_=pst, func=AF.Copy),
                     reads=[("ps", sbk)], writes=[("oT", T_)])

        for tq in range(4):
            for g in range(4):
                hook = (lambda tqp=tq - 1: o_transposes(tqp)) if (g == 0 and tq > 0) else None
                do_group(tq, g, hook)
        o_transposes(3)
        P.barrier()
        self.dbg_reg["oT"] = oT[:, :, :]
        if self.nsa_stop == 2:
            return
        wo = cv(0, 16384, BF16, "p (c n) -> p c n", c=NK)
        wo_v = w_out.rearrange("(c p) n -> p c n", p=128)
        for kq in range(2):
            P.dma("pool", lambda e, kq=kq: e.dma_start(out=wo[:, 4 * kq:4 * kq + 4, :], in_=wo_v[:, 4 * kq:4 * kq + 4, :]),
                  writes=["wo_n"])
        for tt in range(NT):
            for nh in range(2):
                bank = 2 * (tt % 2) + nh
                for c in range(NK):
                    P.op("pe", lambda e, bank=bank, c=c, tt=tt, nh=nh: e.matmul(
                        ps[bank][:, :], lhsT=oT[:, c, tt * 128:(tt + 1) * 128], rhs=wo[:, c, nh * 512:(nh + 1) * 512],
                        start=(c == 0), stop=(c == NK - 1)), reads=[("oT", tt), "wo_n"], writes=[("ps", bank)])
                P.op("dve", lambda e, bank=bank, tt=tt, nh=nh: e.tensor_tensor(
                    out=self.h[:, tt, nh * 512:(nh + 1) * 512], in0=ps[bank][:, :], in1=self.h[:, tt, nh * 512:(nh + 1) * 512],
                    op=ALU.add), reads=[("ps", bank), ("h", tt)], writes=[("h", tt)])
        P.barrier()

    def ffn_prefetch(self, w_in, w_out):
        self.ffn(w_in, w_out, prefetch_only=True)

    def ffn(self, w_in, w_out, prefetch_only=False):
        P = self.P
        C = 4
        wa = [self.carve(b * 8192, 8192, BF16, "p (k n) -> p k n", k=NK) for b in range(2)]
        wb = [self.carve(16384 + b * 8192, 8192, BF16, "p (k n) -> p k n", k=NK) for b in range(2)]
        wo = [self.carve(32768 + b * 8192, 8192, BF16, "p (j n) -> p j n", j=C) for b in range(2)]
        actT = self.carve(49152, 16384, BF16, "p (j t) -> p j t", j=C)
        sa = [self.carve(65536 + b * 2048, 2048, F32) for b in range(2)]
        w_in_v = w_in.rearrange("(k p) n -> p k n", p=128)
        w_out_v = w_out.rearrange("(j p) n -> p j n", p=128)
        groups = [(f0, min(C, NFC - f0)) for f0 in range(0, NFC, C)]

        def load(gi):
            f0, c = groups[gi]
            b = gi % 2
            for half in range(2):
                ks = slice(4 * half, 4 * half + 4)
                P.dma("pool", lambda e, b=b, ks=ks, f0=f0, c=c: e.dma_start(
                    out=wa[b][:, ks, 0:c * 128], in_=w_in_v[:, ks, f0 * 128:(f0 + c) * 128]), writes=[("wa", b)])
                P.dma("pool", lambda e, b=b, ks=ks, f0=f0, c=c: e.dma_start(
                    out=wb[b][:, ks, 0:c * 128], in_=w_in_v[:, ks, FF + f0 * 128:FF + (f0 + c) * 128]),
                    writes=[("wb", b)])
            P.dma("pool", lambda e, b=b, f0=f0, c=c: e.dma_start(
                out=wo[b][:, 0:c, :], in_=w_out_v[:, f0:f0 + c, :]), writes=[("wo", b)])

        def second(gi, tc):
            f0, c = groups[gi]
            b = gi % 2
            for tt in range(4 * tc, 4 * tc + 4):
                yi = tt % 2
                for nh in range(2):
                    bank = 4 + 2 * yi + nh
                    for j in range(c):
                        P.op("pe", lambda e, bank=bank, j=j, tt=tt, nh=nh, b=b, c=c: e.matmul(
                            self.ps[bank][:, :], lhsT=actT[:, j, tt * 128:(tt + 1) * 128],
                            rhs=wo[b][:, j, nh * 512:(nh + 1) * 512], start=(j == 0), stop=(j == c - 1)),
                            reads=[("actT", j, tc), ("wo", b)], writes=[("ps", bank)])
                    P.op("dve", lambda e, bank=bank, tt=tt, nh=nh: e.scalar_tensor_tensor(
                        out=self.h[:, tt, nh * 512:(nh + 1) * 512], in0=self.ps[bank][:, :], scalar=0.5,
                        in1=self.h[:, tt, nh * 512:(nh + 1) * 512], op0=ALU.mult, op1=ALU.add),
                        reads=[("ps", bank), ("h", tt)], writes=[("h", tt)])

        if prefetch_only:
            load(0)
            return
        cnt = 0
        for gi, (f0, c) in enumerate(groups):
            b = gi % 2
            if gi + 1 < len(groups):
                load(gi + 1)
            for tc in range(4):
                for j in range(c):
                    idx = cnt % 2
                    cnt += 1
                    ba, bb = 2 * idx, 2 * idx + 1
                    for (bank, wt, wn) in ((ba, wa, "wa"), (bb, wb, "wb")):
                        for kc in range(NK):
                            P.op("pe", lambda e, bank=bank, wt=wt, kc=kc, j=j, tc=tc, b=b: e.matmul(
                                self.ps[bank][:, :], lhsT=wt[b][:, kc, j * 128:(j + 1) * 128],
                                rhs=self.xT[:, kc, tc * 512:(tc + 1) * 512], start=(kc == 0), stop=(kc == NK - 1)),
                                reads=[(wn, b), ("xT", tc)], writes=[("ps", bank)])
                    P.op("act", lambda e, ba=ba, idx=idx: e.activation(out=sa[idx], in_=self.ps[ba][:, :], func=AF.Silu),
                         reads=[("ps", ba)], writes=[("sa", idx)])
                    P.op("dve", lambda e, bb=bb, idx=idx, j=j, tc=tc: e.tensor_tensor(
                        out=actT[:, j, tc * 512:(tc + 1) * 512], in0=sa[idx], in1=self.ps[bb][:, :], op=ALU.mult),
                        reads=[("sa", idx), ("ps", bb)], writes=[("actT", j, tc)])
                if tc >= 1:
                    second(gi, tc - 1)
            second(gi, 3)


_NC_CACHE = {}


def _get_nc(stages):
    if stages not in _NC_CACHE:
        _NC_CACHE[stages] = Builder(stages).build()
    return _NC_CACHE[stages]


def kernel(**inputs):
    stages = int(os.environ.get("MK_STAGES", "12"))
    nc = _get_nc(stages)
    x = np.asarray(inputs["x"], dtype=np.float32)
    shared = {}
    for k, v in inputs.items():
        if k == "x":
            continue
        a = np.ascontiguousarray(np.asarray(v, dtype=np.float32))
        if k == "kv_norm":
            a = a.reshape(1, D)
        shared[k] = a
    for k, v in _consts().items():
        shared["c_" + k] = v
    ncores = int(os.environ.get("MK_CORES", "8"))
    in_maps = []
    for b in range(ncores):
        m = dict(shared)
        m["x"] = np.ascontiguousarray(x[b])
        in_maps.append(m)
    res = run_bass_kernel_spmd(nc, in_maps, core_ids=list(range(ncores)))
    if os.environ.get("MK_DBG"):
        global DBG_OUT
        DBG_OUT = {k: np.asarray(v) for k, v in res.results[0].items()}
    return np.stack([np.asarray(r["out"]) for r in res.results], axis=0).astype(np.float32)
```

```python
import os
from contextlib import ExitStack

import numpy as np
import concourse.bass as bass
import concourse.mybir as mybir
from concourse.bass_utils import run_bass_kernel_spmd

F32 = mybir.dt.float32
BF16 = mybir.dt.bfloat16
AF = mybir.ActivationFunctionType
ALU = mybir.AluOpType
AX = mybir.AxisListType

S = 2048
D = 1024
FF = 2816
NT = 16
NK = 8
NFC = 22
EPS = 1e-6
NEGM = -30000.0


class _Op:
    __slots__ = ("id", "eng", "fn", "deps", "dma", "sem", "val", "milestone", "is_out")


class Prog:
    ENGS = ("pe", "act", "dve", "pool", "sp")

    def __init__(self, nc, es, n_dma_sems=40):
        self.nc = nc
        self.ops = []
        self.stream = {e: [] for e in self.ENGS}
        self.lastw = {}
        self.readers = {}
        self.engsem = {e: es.enter_context(nc.semaphore("s_" + e)) for e in ("pe", "act", "dve", "pool")}
        self.dsems = [es.enter_context(nc.semaphore("d%d" % i)) for i in range(n_dma_sems)]
        self.dsem_last = [None] * n_dma_sems
        self.dsem_cnt = [0] * n_dma_sems
        self.dnext = 0
        self.pending_barrier = {}

    def _new(self, eng, fn, reads, writes, dma):
        op = _Op()
        op.id = len(self.ops)
        op.eng = eng
        op.fn = fn
        op.dma = dma
        op.milestone = False
        op.is_out = False
        op.sem = None
        op.val = 0
        deps = set()
        ops = self.ops
        for t in reads:
            w = self.lastw.get(t)
            if w is not None:
                if dma or ops[w].dma or ops[w].eng != eng or eng != "pe":
                    deps.add(w)
        for t in writes:
            w = self.lastw.get(t)
            if w is not None and (dma or ops[w].dma or ops[w].eng != eng):
                deps.add(w)
            rd = self.readers.get(t)
            if rd:
                for r in rd.values():
                    if isinstance(r, list):
                        deps.update(r)
                    elif dma or ops[r].eng != eng:
                        deps.add(r)
        pb = self.pending_barrier.pop(eng, None)
        if pb:
            deps.update(pb)
        for t in reads:
            rd = self.readers.setdefault(t, {})
            if dma:
                rd.setdefault("dma", []).append(op.id)
            else:
                rd[eng] = op.id
        for t in writes:
            self.lastw[t] = op.id
            self.readers[t] = {}
        op.deps = deps
        self.ops.append(op)
        self.stream[eng].append(op)
        return op

    def op(self, eng, fn, reads=(), writes=()):
        return self._new(eng, fn, reads, writes, False)

    def dma(self, eng, fn, reads=(), writes=(), is_out=False):
        op = self._new(eng, fn, reads, writes, True)
        k = self.dnext
        self.dnext = (k + 1) % len(self.dsems)
        prev = self.dsem_last[k]
        if prev is not None:
            op.deps.add(prev)
        self.dsem_cnt[k] += 1
        op.sem = self.dsems[k]
        op.val = 16 * self.dsem_cnt[k]
        self.dsem_last[k] = op.id
        op.is_out = is_out
        return op

    def barrier(self):
        b = set()
        for e in self.ENGS:
            for op in reversed(self.stream[e]):
                if not op.dma:
                    b.add(op.id)
                    break
        for k in self.dsem_last:
            if k is not None:
                b.add(k)
        for e in self.ENGS:
            self.pending_barrier[e] = set(b) | self.pending_barrier.get(e, set())

    def emit(self):
        ops = self.ops
        for op in ops:
            for d in op.deps:
                ops[d].milestone = True
        cnt = {e: 0 for e in self.ENGS}
        for op in ops:
            if not op.dma and op.milestone:
                cnt[op.eng] += 1
                op.val = cnt[op.eng]
                op.sem = self.engsem[op.eng]
        outs = [op for op in ops if op.dma and op.is_out]

        def body_for(eng):
            def body(e):
                waited = {}
                for op in self.stream[eng]:
                    need = {}
                    for d in op.deps:
                        dd = ops[d]
                        key = id(dd.sem)
                        if need.get(key, (None, 0))[1] < dd.val:
                            need[key] = (dd.sem, dd.val)
                    for key, (sem, v) in need.items():
                        if waited.get(key, 0) < v:
                            e.wait_ge(sem, v)
                            waited[key] = v
                    ins = op.fn(e)
                    if op.dma:
                        ins.then_inc(op.sem, 16)
                    elif op.milestone:
                        ins.then_inc(op.sem, 1)
                if eng == "sp":
                    for o in outs:
                        e.wait_ge(o.sem, o.val)
            return body

        with self.nc.Block() as block:
            block.tensor(body_for("pe"))
            block.scalar(body_for("act"))
            block.vector(body_for("dve"))
            block.gpsimd(body_for("pool"))
            block.sync(body_for("sp"))


def _consts():
    c = {}
    c["ident"] = np.eye(128, dtype=np.float32)
    sidx = np.arange(128)[:, None]
    tidx = np.arange(128)[None, :]
    c["trin"] = -(sidx >= tidx).astype(np.float32)
    c["onesn"] = -np.ones((128, 128), np.float32)
    c["sbneg"] = np.where(sidx >= tidx, NEGM, 0.0).astype(np.float32)
    c["sb01"] = (sidx < tidx).astype(np.float32)
    c["nsneg"] = np.where(sidx > tidx, NEGM, 0.0).astype(np.float32)
    c["farneg"] = np.where(sidx <= tidx, NEGM, 0.0).astype(np.float32)
    jj = np.arange(32)[:, None]
    ss_ = np.arange(S)[None, :]
    c["esel"] = np.where(ss_ // 64 == jj, NEGM, 0.0).astype(np.float32)
    cc = np.arange(128)[:, None]
    tt_ = np.arange(S)[None, :]
    c["cmask"] = np.where((16 * cc + 31 > tt_) | (cc == 127), NEGM, 0.0).astype(np.float32)
    t = np.arange(S)
    cur = (t // 64)[:, None]
    j = np.arange(32)[None, :]
    forced = (j == 0) | ((cur - j >= 0) & (cur - j < 2))
    m1 = ((~forced) & (j <= cur)).astype(np.float32)
    m2 = np.where(forced, 1e4, np.where(j <= cur, 0.0, -1e4)).astype(np.float32)
    c["m1"] = np.ascontiguousarray(m1.reshape(16, 128, 32).transpose(1, 0, 2))
    c["m2"] = np.ascontiguousarray(m2.reshape(16, 128, 32).transpose(1, 0, 2))
    cst = np.arange(128)[:, None] * 16
    jst = np.arange(32)[None, :] * 64
    c["ovl"] = ((cst < jst + 64) & (cst + 32 > jst)).astype(np.float32)
    c["ovl"][127, :] = 0.0
    inv = np.power(np.float32(500000.0), -np.arange(0, 16, 2, dtype=np.float32) / np.float32(16)).astype(np.float32)
    ang = t.astype(np.float32)[:, None] * inv[None, :]
    c["cos"] = np.ascontiguousarray(np.cos(ang).astype(np.float32).reshape(16, 128, 8).transpose(1, 0, 2))
    c["sin"] = np.ascontiguousarray(np.sin(ang).astype(np.float32).reshape(16, 128, 8).transpose(1, 0, 2))
    angc = (np.arange(128) * 16 + 31).astype(np.float32)[:, None] * inv[None, :]
    c["cosc"] = np.cos(angc).astype(np.float32)
    c["sinc"] = np.sin(angc).astype(np.float32)
    return c


class Builder:
    def __init__(self, stages):
        self.stages = stages
        self.nc = bass.Bass("TRN2", target_bir_lowering=False)
        self.es = ExitStack()

    def dram_in(self, name, shape):
        return self.nc.dram_tensor(name, list(shape), F32, kind="ExternalInput").ap()

    def sb(self, name, shape, dt):
        return self.es.enter_context(self.nc.sbuf_tensor(name, list(shape), dt))

    def build(self):
        nc, es = self.nc, self.es
        with es:
            self._build()
        return nc

    def carve(self, off_bytes, nbytes, dt, pattern=None, **kw):
        assert off_bytes % 4 == 0 and nbytes % 4 == 0
        assert off_bytes + nbytes <= self.arena_bytes, (off_bytes, nbytes)
        v = self.arena[:, off_bytes // 4:(off_bytes + nbytes) // 4]
        if dt != F32:
            v = v.bitcast(dt)
        if pattern:
            v = v.rearrange(pattern, **kw)
        return v

    def _build(self):
        nc = self.nc
        P = self.P = Prog(nc, self.es)
        di = self.dram_in
        self.x_d = di("x", (S, D))
        self.w = {}
        for name, shape in [
            ("ffn1_norm", (4, D)), ("ffn1_w_in", (4, D, 2 * FF)), ("ffn1_w_out", (4, FF, D)),
            ("mix_norm", (4, D)), ("ffn2_norm", (4, D)), ("ffn2_w_in", (4, D, 2 * FF)), ("ffn2_w_out", (4, FF, D)),
            ("sb_w_qkv", (2, D, 3 * D)), ("sb_w_out", (2, D, D)), ("kv_norm", (1, D)), ("nsa_w_kv", (D, 1536)),
            ("nsa_k_norm", (3, 64)), ("cmp_pos_k", (32, 64)), ("cmp_pos_v", (32, 64)),
            ("cmp_k_w1", (2048, 256)), ("cmp_k_w2", (256, 64)), ("cmp_v_w1", (2048, 256)), ("cmp_v_w2", (256, 64)),
            ("nsa_w_in", (2, D, 1072)), ("nsa_q_norm", (2, 64)), ("nsa_w_out", (2, D, D)),
        ]:
            self.w[name] = di(name, shape)
        self.c_d = {k: di("c_" + k, v.shape) for k, v in _consts().items()}
        self.out_d = nc.dram_tensor("out", [S, D], F32, kind="ExternalOutput").ap()

        self.h = self.sb("h", [128, NT, D], F32)
        self.xT = self.sb("xT", [128, NK, S], BF16)
        self.ident = self.sb("ident", [128, 128], BF16)
        self.ss = self.sb("ss", [128, NT], F32)
        self.rstd = self.sb("rstd", [128, NT], F32)
        self.epsc = self.sb("epsc", [128, 1], F32)
        self.arena_bytes = 110 * 1024
        self.arena = self.sb("arena", [128, self.arena_bytes // 4], F32)
        self.ps = [self.es.enter_context(nc.psum_tensor("ps%d" % i, [128, 512], F32)) for i in range(8)]
        self.norm_count = 0
        self.dbg_reg = {}
        self.nsa_stop = int(os.environ.get("MK_NSA_STOP", "0"))
        self.cb = {}
        for k in ("trin", "onesn", "sbneg", "sb01"):
            self.cb[k] = self.sb("sbc_" + k, [128, 128], BF16)
            P.dma("pool", lambda e, k=k: e.dma_start(out=self.cb[k][:], in_=self.c_d[k][:, :]), writes=["c_" + k])
        self.onec = self.sb("onec", [128, 1], F32)
        P.op("pool", lambda e: e.memset(self.onec[:], 1.0), writes=["onec"])

        P.dma("pool", lambda e: e.dma_start(out=self.ident[:], in_=self.c_d["ident"][:, :]), writes=["ident"])
        P.op("pool", lambda e: e.memset(self.epsc[:], EPS), writes=["epsc"])
        xv = self.x_d.rearrange("(t p) d -> p t d", p=128)
        for q in range(4):
            P.dma("sp", lambda e, q=q: e.dma_start(out=self.h[:, 4 * q:4 * q + 4, :], in_=xv[:, 4 * q:4 * q + 4, :]),
                  writes=[("h", t) for t in range(4 * q, 4 * q + 4)])

        st = self.stages
        n = 0
        prev = None
        for layer in range(4):
            for sub in ("a", "b", "c"):
                if n >= st:
                    break
                n += 1
                if sub in ("a", "c"):
                    pre = "ffn1" if sub == "a" else "ffn2"
                    w_in, w_out = self.w[pre + "_w_in"][layer], self.w[pre + "_w_out"][layer]
                    if prev != "ffn":
                        P.barrier()
                    self.ffn_prefetch(w_in, w_out)
                    self.norm_T(self.w[pre + "_norm"][layer:layer + 1, :])
                    self.ffn(w_in, w_out)
                    prev = "ffn"
                    if sub == "c" and layer == 1:
                        P.barrier()
                        self.nsa_kv()
                        prev = "mix"
                elif layer < 2:
                    if prev != "ffn":
                        P.barrier()
                    self.norm_T(self.w["mix_norm"][layer:layer + 1, :])
                    P.barrier()
                    self.sb_mixer(self.w["sb_w_qkv"][layer], self.w["sb_w_out"][layer])
                    prev = "mix"
                else:
                    i = layer - 2
                    if prev != "ffn":
                        P.barrier()
                    self.norm_T(self.w["mix_norm"][layer:layer + 1, :])
                    P.barrier()
                    self.nsa_mixer(self.w["nsa_w_in"][i], self.w["nsa_q_norm"][i:i + 1, :], self.w["nsa_w_out"][i])
                    prev = "mix"
        self.dbg_names = []
        if os.environ.get("MK_DBG"):
            kvv = dict(self.kv_views())
            kvv.update(self.dbg_reg)
            for nm in os.environ["MK_DBG"].split(","):
                ap = kvv[nm]
                shp = list(ap.shape)
                dd = nc.dram_tensor("dbg_" + nm, shp, F32, kind="ExternalOutput").ap()
                P.barrier()
                if len(shp) == 4:
                    for bi in range(shp[1]):
                        P.dma("pool", lambda e, dd=dd, ap=ap, bi=bi: e.dma_start(out=dd[:, bi], in_=ap[:, bi]), is_out=True)
                else:
                    P.dma("pool", lambda e, dd=dd, ap=ap: e.dma_start(out=dd, in_=ap), is_out=True)
                self.dbg_names.append("dbg_" + nm)
        ov = self.out_d.rearrange("(t p) d -> p t d", p=128)
        for q in range(4):
            P.dma("sp", lambda e, q=q: e.dma_start(out=ov[:, 4 * q:4 * q + 4, :], in_=self.h[:, 4 * q:4 * q + 4, :]),
                  reads=[("h", t) for t in range(4 * q, 4 * q + 4)], is_out=True)
        P.emit()

    def norm_T(self, g_row):
        P = self.P
        i = 0
        NB = 69632
        gbc = self.carve(NB, 4096, F32)
        self.hn = [self.carve(NB + 4096 + 2048 * b, 2048, BF16) for b in range(2)]
        self.junk = self.hn[0]
        P.dma("sp", lambda e: e.dma_start(out=gbc[:], in_=g_row.partition_broadcast(128)), writes=[("gbc", i)])
        for tt in range(NT):
            P.op("act", lambda e, tt=tt: e.activation(out=self.junk[:], in_=self.h[:, tt, :], func=AF.Square,
                                                      accum_out=self.ss[:, tt:tt + 1]),
                 reads=[("h", tt)], writes=[("hn", 0), ("ss", tt)])
        P.op("act", lambda e: e.activation(out=self.rstd[:], in_=self.ss[:], func=AF.Ln, scale=1.0 / D,
                                           bias=self.epsc[:]),
             reads=[("ss", t) for t in range(NT)] + ["epsc"], writes=["rstd"])
        P.op("act", lambda e: e.activation(out=self.rstd[:], in_=self.rstd[:], func=AF.Exp, scale=-0.5),
             reads=["rstd"], writes=["rstd"])
        for tt in range(NT):
            hb = self.hn[tt % 2]
            P.op("dve", lambda e, tt=tt, hb=hb: e.scalar_tensor_tensor(out=hb[:], in0=self.h[:, tt, :],
                                                                      scalar=self.rstd[:, tt:tt + 1], in1=gbc[:],
                                                                      op0=ALU.mult, op1=ALU.mult),
                 reads=[("h", tt), "rstd", ("gbc", i)], writes=[("hn", tt % 2)])
            bank = 6 + (tt % 2)
            pst = self.ps[bank][:, :].bitcast(BF16).rearrange("p (k t) -> p k t", k=NK)
            for kc in range(NK):
                P.op("pe", lambda e, kc=kc, hb=hb, pst=pst: e.transpose(pst[:, kc, :], hb[:, kc * 128:(kc + 1) * 128],
                                                                         self.ident[:]),
                     reads=[("hn", tt % 2), "ident"], writes=[("ps", bank)])
            P.op("act", lambda e, tt=tt, pst=pst: e.activation(out=self.xT[:, :, tt * 128:(tt + 1) * 128], in_=pst,
                                                              func=AF.Copy),
                 reads=[("ps", bank)], writes=[("xT", tt // 4)])

    def sb_mixer(self, w_qkv, w_out):
        P = self.P
        cv = self.carve
        qT = [cv(b * 4096, 4096, BF16) for b in range(2)]
        kT = [cv(8192 + b * 4096, 4096, BF16) for b in range(2)]
        vv = [cv(16384 + b * 8192, 8192, BF16, "p (b h n) -> p b h n", b=NT, h=2) for b in range(2)]
        wq = [cv(32768 + b * 6144, 6144, BF16, "p (k w n) -> p k w n", k=NK, w=3) for b in range(2)]
        oT = cv(45056, 32768, BF16, "p (c t) -> p c t", c=NK)
        ee = cv(77824, 2048, F32)
        spb = [cv(79872 + b * 1024, 1024, BF16) for b in range(2)]
        rs32 = cv(81920, 2048, F32)
        rsb = [cv(83968 + b * 1024, 1024, BF16) for b in range(2)]
        wb = [cv(86016 + b * 1024, 1024, BF16) for b in range(2)]
        wo2 = [cv(88064, 8192, BF16, "p (c n) -> p c n", c=4), cv(99328, 8192, BF16, "p (c n) -> p c n", c=4)]
        assert 99328 + 8192 <= self.arena_bytes
        ps = self.ps
        wq_v = w_qkv.rearrange("(k p) (w n) -> p k w n", p=128, w=3)
        wo_v = w_out.rearrange("(c p) n -> p c n", p=128)
        ident = self.ident

        for b in range(2):
            P.op("pool", lambda e, b=b: e.memset(vv[b][:, :, :, :], 0.0), writes=[("v", b)])

        def load_w(p):
            b = p % 2
            for w3 in range(3):
                P.dma("pool", lambda e, w3=w3: e.dma_start(out=wq[b][:, :, w3, :],
                                                           in_=wq_v[:, :, w3, p * 128:(p + 1) * 128]),
                      writes=[("wq", b)])

        def proj_pieces(p):
            b = p % 2
            pieces = []
            for which, dst, nm in ((0, qT, "qT"), (1, kT, "kT")):
                for tc in range(4):
                    def piece(which=which, dst=dst, nm=nm, tc=tc):
                        bank = 6 + (tc % 2)
                        for kc in range(NK):
                            P.op("pe", lambda e, kc=kc: e.matmul(
                                ps[bank][:, :], lhsT=wq[b][:, kc, which, :], rhs=self.xT[:, kc, tc * 512:(tc + 1) * 512],
                                start=(kc == 0), stop=(kc == NK - 1)),
                                reads=[("wq", b), ("xT", tc)], writes=[("ps", bank)])
                        if which == 0:
                            P.op("dve", lambda e: e.tensor_scalar(
                                out=dst[b][:, tc * 512:(tc + 1) * 512], in0=ps[bank][:, :], scalar1=0.125, scalar2=None,
                                op0=ALU.mult), reads=[("ps", bank)], writes=[(nm, b)])
                        else:
                            P.op("dve", lambda e: e.tensor_copy(out=dst[b][:, tc * 512:(tc + 1) * 512],
                                                                in_=ps[bank][:, :]),
                                 reads=[("ps", bank)], writes=[(nm, b)])
                    pieces.append(piece)
            for t4 in range(4):
                def piece(t4=t4):
                    bank = 6 + (t4 % 2)
                    for i in range(4):
                        tt = 4 * t4 + i
                        for kc in range(NK):
                            P.op("pe", lambda e, kc=kc, tt=tt, i=i: e.matmul(
                                ps[bank][:, i * 128:(i + 1) * 128], lhsT=self.xT[:, kc, tt * 128:(tt + 1) * 128],
                                rhs=wq[b][:, kc, 2, :], start=(kc == 0), stop=(kc == NK - 1)),
                                reads=[("wq", b), ("xT", tt // 4)], writes=[("ps", bank)])
                    pv = ps[bank][:, :].rearrange("p (i n) -> p i n", i=4)
                    for hh in range(2):
                        P.op("dve", lambda e, hh=hh: e.tensor_copy(
                            out=vv[b][:, 4 * t4:4 * t4 + 4, hh, 64 * hh:64 * hh + 64], in_=pv[:, :, 64 * hh:64 * hh + 64]),
                            reads=[("ps", bank)], writes=[("v", b)])
                pieces.append(piece)
            return pieces

        ee2 = [ee, cv(96256, 2048, F32)]
        sp3 = [spb[0], spb[1], cv(98304, 1024, BF16)]
        its = []
        for p in range(NK):
            for tq in range(4):
                for hh in range(2):
                    bmax = 4 * tq + 3
                    for blk in range(bmax, -1, -1):
                        its.append((p, tq, hh, blk))

        def geom(it):
            p, tq, hh, blk = it
            t0 = max(512 * tq, 128 * blk)
            N = 512 * (tq + 1) - t0
            c0 = t0 - 512 * tq
            return t0, N, c0, (blk >= 4 * tq), (blk == 4 * tq + 3)

        def slices(it):
            p, tq, hh, blk = it
            b = p % 2
            t0, N, c0, diag, first = geom(it)
            pr = slice(64 * hh, 64 * hh + 64)
            return kT[b][pr, blk * 128:(blk + 1) * 128], qT[b][pr, t0:t0 + N]

        HEAT = int(os.environ.get("MK_HEAT", "0"))
        NZ = 4

        def unit_of(n):
            n = min(max(n, 0), len(its) - 1)
            return 4 * its[n][0] + its[n][1]

        def heater(k):
            us = {unit_of(j) for j in range(k - 7, k + 1)}
            if len(us) != 1:
                return
            hb = 4 + ((us.pop() + 1) % 2)
            for _ in range(HEAT):
                P.op("pe", lambda e: e.matmul(ps[hb][:, :], lhsT=ident[:], rhs=self.xT[:, 0, 0:512], start=True, stop=True,
                                              skip_group_check=True), reads=["ident", ("xT", 0)], writes=[("ps", hb)])

        def st0(n):
            it = its[n]
            p = it[0]
            b = p % 2
            t0, N, c0, diag, first = geom(it)
            kslc, qslc = slices(it)
            zb = n % NZ
            P.op("pe", lambda e: e.matmul(ps[zb][:, 0:N], lhsT=kslc, rhs=qslc, start=True, stop=False, skip_group_check=True),
                 reads=[("kT", b), ("qT", b)], writes=[("ps", zb)])
            P.op("act", lambda e: e.activation(out=ee2[n % 2][:, 0:N], in_=ps[zb][:, 0:N], func=AF.Exp),
                 reads=[("ps", zb)], writes=[("ee", n % 2)])

        def st1(n):
            it = its[n]
            t0, N, c0, diag, first = geom(it)
            i3 = n % 3
            P.op("act", lambda e: e.activation(out=sp3[i3][:, 0:N], in_=ee2[n % 2][:, 0:N], func=AF.Ln, bias=self.onec[:]),
                 reads=[("ee", n % 2), "onec"], writes=[("sp", i3)])
            if diag:
                P.op("pool", lambda e: e.tensor_tensor(out=sp3[i3][:, 0:128], in0=sp3[i3][:, 0:128],
                                                       in1=self.cb["sb01"][:], op=ALU.mult),
                     reads=[("sp", i3), "c_sb01"], writes=[("sp", i3)])
            if first and c0 > 0:
                P.op("pool", lambda e: e.memset(rs32[:, 0:c0], 0.0), writes=["rs32"])

        def st2(n):
            it = its[n]
            p, tq, hh, blk = it
            b = p % 2
            t0, N, c0, diag, first = geom(it)
            kslc, qslc = slices(it)
            i3 = n % 3
            i2 = n % 2
            ab = n % NZ
            nacc = 2 + (0 if first else 1) + (1 if diag else 0)
            P.op("pe", lambda e: e.matmul(ps[ab][:, 0:N], lhsT=self.cb["trin"][:], rhs=sp3[i3][:, 0:N], start=False,
                                          stop=(nacc == 2), skip_group_check=True),
                 reads=["c_trin", ("sp", i3)], writes=[("ps", ab)])
            if not first:
                P.op("pe", lambda e: e.matmul(ps[ab][:, 0:N], lhsT=self.cb["onesn"][:], rhs=rsb[i2][:, c0:c0 + N], start=False,
                                              stop=(not diag), skip_group_check=True),
                     reads=["c_onesn", ("rsb", i2)], writes=[("ps", ab)])
            if diag:
                P.op("pe", lambda e: e.matmul(ps[ab][:, 0:128], lhsT=ident[:], rhs=self.cb["sbneg"][:], start=False, stop=True,
                                              skip_group_check=True), reads=["ident", "c_sbneg"], writes=[("ps", ab)])
            P.op("act", lambda e: e.activation(out=wb[i2][:, 0:N], in_=ps[ab][:, 0:N], func=AF.Exp),
                 reads=[("ps", ab)], writes=[("w", i2)])
            if blk > 0:
                if first:
                    P.op("dve", lambda e: e.tensor_copy(out=rs32[:, c0:c0 + N], in_=sp3[i3][:, 0:N]),
                         reads=[("sp", i3)], writes=["rs32"])
                else:
                    P.op("dve", lambda e: e.tensor_tensor(out=rs32[:, c0:c0 + N], in0=rs32[:, c0:c0 + N], in1=sp3[i3][:, 0:N],
                                                          op=ALU.add), reads=[("sp", i3), "rs32"], writes=["rs32"])
                c0n = max(512 * tq, 128 * (blk - 1)) - 512 * tq
                inx = (n + 1) % 2
                P.op("dve", lambda e: e.tensor_copy(out=rsb[inx][:, c0n:512], in_=rs32[:, c0n:512]),
                     reads=["rs32"], writes=[("rsb", inx)])

        def st3(n):
            it = its[n]
            p, tq, hh, blk = it
            b = p % 2
            t0, N, c0, diag, first = geom(it)
            i2 = n % 2
            ob = 4 + ((4 * p + tq) % 2)
            fm = (hh == 0 and first)
            lm = (hh == 1 and blk == 0)
            P.op("pe", lambda e: e.matmul(ps[ob][:, c0:c0 + N], lhsT=vv[b][:, blk, hh, :], rhs=wb[i2][:, 0:N], start=fm, stop=lm,
                                          skip_group_check=True), reads=[("v", b), ("w", i2)], writes=[("ps", ob)])
            if lm:
                P.op("dve", lambda e: e.tensor_copy(out=oT[:, p, tq * 512:(tq + 1) * 512], in_=ps[ob][:, :]),
                     reads=[("ps", ob)], writes=[("oT", p)])

        load_w(0)
        load_w(1)
        for half in range(2):
            P.dma("pool", lambda e, half=half: e.dma_start(out=wo2[half][:, :, :], in_=wo_v[:, 4 * half:4 * half + 4, :]),
                  writes=[("wo_sb", half)])
        for pc in proj_pieces(0):
            pc()
        nit = len(its)
        pieces = []
        cur_p = -1
        for k in range(nit + 3):
            if k < nit and its[k][0] != cur_p:
                cur_p = its[k][0]
                while pieces:
                    pieces.pop(0)()
                if cur_p + 1 < NK:
                    pieces = proj_pieces(cur_p + 1)
                if cur_p + 2 < NK:
                    load_w(cur_p + 2)
            if k < nit:
                st0(k)
            if 0 <= k - 1 < nit:
                st1(k - 1)
            if 0 <= k - 2 < nit:
                st2(k - 2)
            heater(k)
            if 0 <= k - 3 < nit:
                st3(k - 3)
            if pieces and k % 5 == 4:
                pieces.pop(0)()
        for half in range(2):
            wo = wo2[half]
            for tt in range(NT):
                for nh in range(2):
                    bank = 2 * (tt % 2) + nh
                    for c in range(4):
                        P.op("pe", lambda e, bank=bank, c=c, tt=tt, nh=nh, half=half: e.matmul(
                            ps[bank][:, :], lhsT=oT[:, 4 * half + c, tt * 128:(tt + 1) * 128],
                            rhs=wo2[half][:, c, nh * 512:(nh + 1) * 512], start=(c == 0), stop=(c == 3)),
                            reads=[("oT", 4 * half + c), ("wo_sb", half)], writes=[("ps", bank)])
                    P.op("dve", lambda e, bank=bank, tt=tt, nh=nh: e.tensor_tensor(
                        out=self.h[:, tt, nh * 512:(nh + 1) * 512], in0=ps[bank][:, :],
                        in1=self.h[:, tt, nh * 512:(nh + 1) * 512], op=ALU.add),
                        reads=[("ps", bank), ("h", tt)], writes=[("h", tt)])
        P.barrier()

    def head_norm_rope_gen(self, src, src_tok, H, gain, gain_tok, cos_t, sin_t, wk, tag, finish):
        P = self.P
        sq, qn, ssq, t1, t2, t3, t4 = wk["sq"], wk["qn"], wk["ssq"], wk["t1"], wk["t2"], wk["t3"], wk["t4"]
        n = H * 64
        T = lambda nm: (tag, nm)
        src3 = src.rearrange("p (h d) -> p h d", h=H)
        sq3 = sq[:, 0:n].rearrange("p (h d) -> p h d", h=H)
        qn3 = qn[:, 0:n].rearrange("p (h d) -> p h d", h=H)
        P.op("act", lambda e: e.activation(out=sq[:, 0:n], in_=src, func=AF.Square), reads=[src_tok], writes=[T("sq")])
        yield
        P.op("dve", lambda e: e.tensor_reduce(out=ssq[:, 0:H], in_=sq3, axis=AX.X, op=ALU.add),
             reads=[T("sq")], writes=[T("ssq")])
        yield
        P.op("act", lambda e: e.activation(out=ssq[:, 0:H], in_=ssq[:, 0:H], func=AF.Ln, scale=1.0 / 64,
                                           bias=self.epsc[:]), reads=[T("ssq"), "epsc"], writes=[T("ssq")])
        yield
        P.op("act", lambda e: e.activation(out=ssq[:, 0:H], in_=ssq[:, 0:H], func=AF.Exp, scale=-0.5),
             reads=[T("ssq")], writes=[T("ssq")])
        yield
        P.op("dve", lambda e: e.tensor_tensor(out=qn3, in0=src3, in1=ssq[:, 0:H].unsqueeze(2).to_broadcast([128, H, 64]),
                                              op=ALU.mult), reads=[src_tok, T("ssq")], writes=[T("qn")])
        yield
        P.op("dve", lambda e: e.tensor_tensor(out=qn3, in0=qn3, in1=gain.unsqueeze(1).to_broadcast([128, H, 64]),
                                              op=ALU.mult), reads=[T("qn"), gain_tok], writes=[T("qn")])
        yield
        x1 = qn3[:, :, 0:8]
        x2 = qn3[:, :, 8:16]
        cb = cos_t.unsqueeze(1).to_broadcast([128, H, 8])
        sb_ = sin_t.unsqueeze(1).to_broadcast([128, H, 8])
        tv = [t[:, 0:H * 8].rearrange("p (h d) -> p h d", h=H) for t in (t1, t2, t3, t4)]
        P.op("pool", lambda e: e.tensor_tensor(out=tv[0], in0=x1, in1=cb, op=ALU.mult), reads=[T("qn"), "rope"], writes=[T("t1")])
        P.op("pool", lambda e: e.tensor_tensor(out=tv[1], in0=x2, in1=sb_, op=ALU.mult), reads=[T("qn"), "rope"], writes=[T("t2")])
        P.op("dve", lambda e: e.tensor_tensor(out=tv[2], in0=x2, in1=cb, op=ALU.mult), reads=[T("qn"), "rope"], writes=[T("t3")])
        P.op("dve", lambda e: e.tensor_tensor(out=tv[3], in0=x1, in1=sb_, op=ALU.mult), reads=[T("qn"), "rope"], writes=[T("t4")])
        yield
        P.op("dve", lambda e: e.tensor_tensor(out=x1, in0=tv[0], in1=tv[1], op=ALU.subtract),
             reads=[T("t1"), T("t2"), T("t3"), T("t4")], writes=[T("qn")])
        P.op("dve", lambda e: e.tensor_tensor(out=x2, in0=tv[2], in1=tv[3], op=ALU.add),
             reads=[T("t3"), T("t4"), T("qn")], writes=[T("qn")])
        yield
        finish(qn3)

    @staticmethod
    def interleave(gens):
        gens = list(gens)
        while gens:
            for g_ in list(gens):
                try:
                    next(g_)
                except StopIteration:
                    gens.remove(g_)

    def head_norm_rope(self, src, src_tok, H, gain, gain_tok, cos_t, sin_t, wk, tag):
        res = []
        self.interleave([self.head_norm_rope_gen(src, src_tok, H, gain, gain_tok, cos_t, sin_t, wk, tag, res.append)])
        return res[0]

    def load_gain64(self, dst, row_ap, tok, scale=None):
        P = self.P
        P.dma("sp", lambda e: e.dma_start(out=dst, in_=row_ap.partition_broadcast(128)), writes=[tok])
        if scale is not None:
            P.op("pool", lambda e: e.tensor_scalar(out=dst, in0=dst, scalar1=float(scale), scalar2=None, op0=ALU.mult),
                 reads=[tok], writes=[tok])

    def kv_views(self):
        base = self.arena_bytes - 34816
        cv = self.carve
        v = {}
        v["base"] = base
        v["ksT"] = cv(base, 8192, BF16, "p (c t) -> p c t", c=2)
        v["kwT"] = cv(base + 8192, 8192, BF16, "p (c t) -> p c t", c=2)
        v["vs"] = cv(base + 16384, 8320, BF16, "p (b g n) -> p b g n", b=NT, g=4)
        v["vw"] = cv(base + 24704, 8320, BF16, "p (b g n) -> p b g n", b=NT, g=4)
        v["kcT"] = cv(base + 33024, 512, BF16, "p (c t) -> p c t", c=2)
        v["vc"] = cv(base + 33536, 776, BF16, "p (g n) -> p g n", g=4)
        return v

    def nsa_kv(self):
        P = self.P
        W = self.w
        self.norm_T(W["kv_norm"][0:1, :])
        kv = self.kv_views()
        cv = self.carve
        ps = self.ps
        ident = self.ident
        wkv = cv(0, 24576, BF16, "p (k n) -> p k n", k=NK)
        kcr = cv(24576, 8256, BF16, "p (c t) -> p c t", c=2)
        vcr = cv(32832, 8256, BF16, "p (c t) -> p c t", c=2)
        o = 41088
        wkA = []
        for _ in range(2):
            wk = {}
            for nm, nb in (("sq", 1024), ("qn", 1024), ("ssq", 64), ("t1", 128), ("t2", 128), ("t3", 128), ("t4", 128)):
                wk[nm] = cv(o, nb, F32)
                o += nb
            wkA.append(wk)
        gk = []
        for i in range(3):
            gk.append(cv(o, 256, F32)); o += 256
        cosb = cv(o, 512, F32, "p (t f) -> p t f", t=NT); o += 512
        sinb = cv(o, 512, F32, "p (t f) -> p t f", t=NT); o += 512
        cosc = cv(o, 32, F32); o += 32
        sinc = cv(o, 32, F32); o += 32
        kbf = [cv(o + b * 2048, 2048, BF16) for b in range(2)]
        o += 4096
        offA = o
        wkv_v = W["nsa_w_kv"].rearrange("(k p) n -> p k n", p=128)
        for kq in range(4):
            P.dma("pool", lambda e, kq=kq: e.dma_start(out=wkv[:, 2 * kq:2 * kq + 2, :], in_=wkv_v[:, 2 * kq:2 * kq + 2, :]),
                  writes=["wkv"])
        for i in range(3):
            self.load_gain64(gk[i], W["nsa_k_norm"][i:i + 1, :], ("gk", i))
        P.dma("sp", lambda e: e.dma_start(out=cosb, in_=self.c_d["cos"][:, :, :]), writes=["rope"])
        P.dma("sp", lambda e: e.dma_start(out=sinb, in_=self.c_d["sin"][:, :, :]), writes=["rope"])
        P.dma("sp", lambda e: e.dma_start(out=cosc, in_=self.c_d["cosc"][:, :]), writes=["rope"])
        P.dma("sp", lambda e: e.dma_start(out=sinc, in_=self.c_d["sinc"][:, :]), writes=["rope"])
        for t_, nm in ((kcr, "kcr"), (vcr, "vcr")):
            P.op("pool", lambda e, t_=t_: e.memset(t_[:, :, 2048:2064], 0.0), writes=[nm])
        for nm in ("vs", "vw"):
            P.op("pool", lambda e, nm=nm: e.memset(kv[nm][:, :, :, 64:65], 1.0), writes=[nm])
        P.op("pool", lambda e: e.memset(kv["vc"][:, :, 64:65], 1.0), writes=["vc"])
        for g in range(4):
            P.dma("pool", lambda e, g=g: e.dma_start(out=kv["vc"][:, g, 65:97], in_=self.c_d["ovl"][:, :]), writes=["vc"])

        pendA = []
        for tt in range(NT):
            par = tt % 2
            b0 = 4 * par
            for nb in range(3):
                for kc in range(NK):
                    P.op("pe", lambda e, nb=nb, kc=kc, tt=tt, b0=b0: e.matmul(
                        ps[b0 + nb][:, :], lhsT=self.xT[:, kc, tt * 128:(tt + 1) * 128], rhs=wkv[:, kc, nb * 512:(nb + 1) * 512],
                        start=(kc == 0), stop=(kc == NK - 1)), reads=[("xT", tt // 4), "wkv"], writes=[("ps", b0 + nb)])
            kb = kbf[par]
            while len(pendA) > 1:
                pendA.pop(0)()
            gens = []
            for (nb, gi, off, nm) in ((1, 1, 0, "ks"), (2, 2, 256, "kw")):
                def fin(qn3, off=off, kb=kb, nb=nb, par=par):
                    P.op("dve", lambda e: e.tensor_copy(out=kb[:, off:off + 256].rearrange("p (h d) -> p h d", h=4), in_=qn3),
                         reads=[("kvn%d" % nb, "qn")], writes=[("kbf", par)])
                gens.append(self.head_norm_rope_gen(ps[b0 + nb][:, 0:256], ("ps", b0 + nb), 4, gk[gi], ("gk", gi),
                                                    cosb[:, tt, :], sinb[:, tt, :], wkA[nb - 1], "kvn%d" % nb, fin))
            self.interleave(gens)
            P.op("dve", lambda e, kb=kb, b0=b0: e.tensor_copy(out=kb[:, 512:1024], in_=ps[b0][:, :]),
                 reads=[("ps", b0)], writes=[("kbf", par)])
            for (nb, nm) in ((1, "vs"), (2, "vw")):
                P.op("dve", lambda e, nb=nb, nm=nm, tt=tt, b0=b0: e.tensor_copy(
                    out=kv[nm][:, tt, :, 0:64], in_=ps[b0 + nb][:, 256:512].rearrange("p (g d) -> p g d", g=4)),
                    reads=[("ps", b0 + nb)], writes=[nm])
            def tail(tt=tt, b0=b0, par=par, kb=kb):
                tb = b0 + 3
                pst = ps[tb][:, :].bitcast(BF16).rearrange("p (k t) -> p k t", k=NK)
                for c in range(8):
                    P.op("pe", lambda e, c=c: e.transpose(pst[:, c, :], kb[:, c * 128:(c + 1) * 128], ident[:]),
                         reads=[("kbf", par), "ident"], writes=[("ps", tb)])
                tsl = slice(tt * 128, (tt + 1) * 128)
                for (c0, dst, nm) in ((0, kv["ksT"], "ksT"), (2, kv["kwT"], "kwT"), (4, kcr, "kcr"), (6, vcr, "vcr")):
                    P.op("act", lambda e, c0=c0, dst=dst: e.activation(
                        out=dst[:, :, tsl], in_=pst[:, c0:c0 + 2, :], func=AF.Copy), reads=[("ps", tb)], writes=[nm])
            pendA.append(tail)
        while pendA:
            pendA.pop(0)()
        P.barrier()
        w1d = cv(41088, 16384, BF16, "p (l n) -> p l n", l=32)
        o = 41088 + 16384
        w2b = cv(o, 256, BF16, "p (c n) -> p c n", c=2); o += 256
        posT = cv(o, 64, BF16); o += 64
        bias = cv(o, 8, F32); o += 8
        hT = cv(o, 512, BF16, "p (c n) -> p c n", c=2); o += 512
        wkB = {}
        for nm, nb in (("sq", 1024), ("qn", 1024), ("ssq", 64), ("t1", 128), ("t2", 128), ("t3", 128), ("t4", 128)):
            wkB[nm] = cv(o, nb, F32)
            o += nb
        gkc = cv(o, 256, F32); o += 256
        coscB = cv(o, 32, F32); o += 32
        sincB = cv(o, 32, F32); o += 32
        kcb = cv(o, 512, BF16); o += 512
        assert o <= kv["base"]
        self.load_gain64(gkc, W["nsa_k_norm"][0:1, :], "gkc")
        P.dma("sp", lambda e: e.dma_start(out=coscB, in_=self.c_d["cosc"][:, :]), writes=["ropeB"])
        P.dma("sp", lambda e: e.dma_start(out=sincB, in_=self.c_d["sinc"][:, :]), writes=["ropeB"])
        for X in range(2):
            w1 = W["cmp_k_w1" if X == 0 else "cmp_v_w1"].rearrange("(l d) n -> d l n", d=64)
            w2 = W["cmp_k_w2" if X == 0 else "cmp_v_w2"].rearrange("(c p) n -> p c n", p=128)
            pos = W["cmp_pos_k" if X == 0 else "cmp_pos_v"].rearrange("l d -> d l")
            raw = kcr if X == 0 else vcr
            rawtok = "kcr" if X == 0 else "vcr"
            for hf in range(2):
                for lh in range(2):
                    P.dma("pool", lambda e, hf=hf, lh=lh, w1=w1: e.dma_start(
                        out=w1d[64 * hf:64 * hf + 64, 16 * lh:16 * lh + 16, :], in_=w1[:, 16 * lh:16 * lh + 16, :]),
                        writes=["w1d"])
            P.dma("pool", lambda e, w2=w2: e.dma_start(out=w2b, in_=w2[:, :, :]), writes=["w2b"])
            P.dma("pool", lambda e, pos=pos: e.dma_start(out=posT[0:64, :], in_=pos, allow_slow_non_contiguous=True),
                  writes=["posT"])
            for ncn in range(2):
                for l in range(32):
                    P.op("pe", lambda e, ncn=ncn, l=l: e.matmul(
                        ps[7][:, ncn:ncn + 1], lhsT=w1d[0:64, l, ncn * 128:(ncn + 1) * 128], rhs=posT[0:64, l:l + 1],
                        start=(l == 0), stop=(l == 31)), reads=["w1d", "posT"], writes=[("ps", 7)])
            P.op("dve", lambda e: e.tensor_copy(out=bias, in_=ps[7][:, 0:2]), reads=[("ps", 7)], writes=["cbias"])
            rawv = raw.rearrange("p c (a s) -> p c a s", s=16)
            for g in range(4):
                pr = slice(64 * (g % 2), 64 * (g % 2) + 64)
                pair = g // 2
                for ncn in range(2):
                    bank = ncn
                    for l in range(32):
                        rhs = rawv[pr, pair, 0:128, l] if l < 16 else rawv[pr, pair, 1:129, l - 16]
                        P.op("pe", lambda e, bank=bank, l=l, ncn=ncn, rhs=rhs, pr=pr: e.matmul(
                            ps[bank][:, 0:128], lhsT=w1d[pr, l, ncn * 128:(ncn + 1) * 128], rhs=rhs,
                            start=(l == 0), stop=(l == 31)), reads=["w1d", rawtok], writes=[("ps", bank)])
                    P.op("act", lambda e, bank=bank, ncn=ncn: e.activation(
                        out=hT[:, ncn, :], in_=ps[bank][:, 0:128], func=AF.Silu, bias=bias[:, ncn:ncn + 1]),
                        reads=[("ps", bank), "cbias"], writes=["hT"])
                for ncn in range(2):
                    P.op("pe", lambda e, ncn=ncn, g=g: e.matmul(
                        ps[2][:, g * 64:(g + 1) * 64], lhsT=hT[:, ncn, :], rhs=w2b[:, ncn, :],
                        start=(ncn == 0), stop=(ncn == 1)), reads=["hT", "w2b"], writes=[("ps", 2)])
            if X == 0:
                qn3 = self.head_norm_rope(ps[2][:, 0:256], ("ps", 2), 4, gkc, "gkc", coscB, sincB, wkB, "kcn")
                P.op("dve", lambda e, qn3=qn3: e.tensor_copy(out=kcb.rearrange("p (h d) -> p h d", h=4), in_=qn3),
                     reads=[("kcn", "qn")], writes=["kcb"])
                pst = ps[3][:, :].bitcast(BF16).rearrange("p (k t) -> p k t", k=NK)
                for c in range(2):
                    P.op("pe", lambda e, c=c, pst=pst: e.transpose(pst[:, c, :], kcb[:, c * 128:(c + 1) * 128], ident[:]),
                         reads=["kcb", "ident"], writes=[("ps", 3)])
                P.op("act", lambda e, pst=pst: e.activation(out=kv["kcT"][:, :, :], in_=pst[:, 0:2, :], func=AF.Copy),
                     reads=[("ps", 3)], writes=["kcT"])
            else:
                P.op("dve", lambda e: e.tensor_copy(out=kv["vc"][:, :, 0:64],
                                                    in_=ps[2][:, 0:256].rearrange("p (g d) -> p g d", g=4)),
                     reads=[("ps", 2)], writes=["vc"])
        P.barrier()

    def nsa_mixer(self, w_in, q_norm_row, w_out):
        P = self.P
        kv = self.kv_views()
        cv = self.carve
        ps = self.ps
        ident = self.ident
        qT = cv(0, 32768, BF16, "p (c t) -> p c t", c=NK)
        gates = cv(32768, 3072, F32, "p (t n) -> p t n", t=NT)
        cosb = cv(35840, 512, F32, "p (t f) -> p t f", t=NT)
        sinb = cv(36352, 512, F32, "p (t f) -> p t f", t=NT)
        o0 = 36864
        win = cv(o0, 17152, BF16, "p (k n) -> p k n", k=NK)
        o = o0 + 17152
        wk2 = []
        for _ in range(2):
            wk = {}
            for nm, nb in (("sq", 2048), ("qn", 2048), ("ssq", 64), ("t1", 256), ("t2", 256), ("t3", 256), ("t4", 256)):
                wk[nm] = cv(o, nb, F32)
                o += nb
            wk2.append(wk)
        gq = cv(o, 256, F32); o += 256
        eg = cv(o, 192, F32); o += 192
        qbf = [cv(o + b * 2048, 2048, BF16) for b in range(2)]
        o += 4096
        assert o <= kv["base"], o
        win_v = w_in.rearrange("(k p) n -> p k n", p=128)
        for kq in range(4):
            P.dma("pool", lambda e, kq=kq: e.dma_start(out=win[:, 2 * kq:2 * kq + 2, :], in_=win_v[:, 2 * kq:2 * kq + 2, :]),
                  writes=["win"])
        self.load_gain64(gq, q_norm_row, "gq", scale=0.125)
        P.dma("sp", lambda e: e.dma_start(out=cosb, in_=self.c_d["cos"][:, :, :]), writes=["rope"])
        P.dma("sp", lambda e: e.dma_start(out=sinb, in_=self.c_d["sin"][:, :, :]), writes=["rope"])
        pendQ = []
        for tt in range(NT):
            par = tt % 2
            b0 = 4 * par
            for nb, (c0, c1) in enumerate(((0, 512), (512, 1024), (1024, 1072))):
                for kc in range(NK):
                    P.op("pe", lambda e, nb=nb, kc=kc, c0=c0, c1=c1, tt=tt, b0=b0: e.matmul(
                        ps[b0 + nb][:, 0:c1 - c0], lhsT=self.xT[:, kc, tt * 128:(tt + 1) * 128], rhs=win[:, kc, c0:c1],
                        start=(kc == 0), stop=(kc == NK - 1)), reads=[("xT", tt // 4), "win"], writes=[("ps", b0 + nb)])
            qb = qbf[par]
            while len(pendQ) > 1:
                pendQ.pop(0)()
            qbv = qb.rearrange("p (hs r pr d) -> p hs pr r d", hs=2, r=4, pr=2)
            gens = []
            for hs in range(2):
                def fin(qn3, hs=hs, qbv=qbv, par=par):
                    P.op("dve", lambda e: e.tensor_copy(out=qbv[:, hs], in_=qn3.rearrange("p (pr r) d -> p pr r d", pr=2)),
                         reads=[("qn%d" % hs, "qn")], writes=[("qbf", par)])
                gens.append(self.head_norm_rope_gen(ps[b0 + hs][:, :], ("ps", b0 + hs), 8, gq, "gq", cosb[:, tt, :],
                                                    sinb[:, tt, :], wk2[hs], "qn%d" % hs, fin))
            self.interleave(gens)
            P.op("act", lambda e, b0=b0: e.activation(out=eg[:, 0:48], in_=ps[b0 + 2][:, 0:48], func=AF.Exp, scale=-1.0),
                 reads=[("ps", b0 + 2)], writes=["eg"])
            P.op("dve", lambda e: e.tensor_scalar(out=eg[:, 0:48], in0=eg[:, 0:48], scalar1=1.0, scalar2=None, op0=ALU.add),
                 reads=["eg"], writes=["eg"])
            P.op("dve", lambda e, tt=tt: e.reciprocal(out=gates[:, tt, :], in_=eg[:, 0:48]), reads=["eg"], writes=["gates"])
            def tail(tt=tt, b0=b0, par=par, qb=qb):
                tb = b0 + 3
                pst = ps[tb][:, :].bitcast(BF16).rearrange("p (k t) -> p k t", k=NK)
                for c in range(NK):
                    P.op("pe", lambda e, c=c: e.transpose(pst[:, c, :], qb[:, c * 128:(c + 1) * 128], ident[:]),
                         reads=[("qbf", par), "ident"], writes=[("ps", tb)])
                P.op("act", lambda e: e.activation(out=qT[:, :, tt * 128:(tt + 1) * 128], in_=pst, func=AF.Copy),
                     reads=[("ps", tb)], writes=["qT"])
            pendQ.append(tail)
        while pendQ:
            pendQ.pop(0)()
        P.barrier()
        self.dbg_reg["qT"] = qT
        self.dbg_reg["gates"] = gates
        if self.nsa_stop == 1:
            return
        o = o0
        esel = cv(o, 4096, BF16); o += 4096
        cmask = cv(o, 4096, BF16); o += 4096
        m1 = cv(o, 1024, BF16, "p (t j) -> p t j", t=NT); o += 1024
        m2 = cv(o, 1024, BF16, "p (t j) -> p t j", t=NT); o += 1024
        nsneg = cv(o, 256, BF16); o += 256
        farneg = cv(o, 256, BF16); o += 256
        PT = [cv(o + i * 1024, 1024, BF16) for i in range(12)]; o += 12288
        nselT = cv(o, 1024, BF16); o += 1024
        oacc = cv(o, 4096, F32, "p (i n) -> p i n", i=4); o += 4096
        obf = cv(o, 8192, BF16, "p (i n) -> p i n", i=4); o += 8192
        den = rec = scl = imp = sc2 = m8 = nsel = None
        oT = self.xT
        for (dst, nm, rows) in ((esel, "esel", 32), (cmask, "cmask", 128), (nsneg, "nsneg", 128), (farneg, "farneg", 128)):
            P.dma("pool", lambda e, dst=dst, nm=nm, rows=rows: e.dma_start(out=dst[0:rows, :], in_=self.c_d[nm][:, :]),
                  writes=["c_" + nm])
        P.dma("pool", lambda e: e.dma_start(out=m1, in_=self.c_d["m1"][:, :, :]), writes=["m1"])
        P.dma("pool", lambda e: e.dma_start(out=m2, in_=self.c_d["m2"][:, :, :]), writes=["m2"])
        SB_ = (0, 1, 2, 3)
        OB = (4, 5, 6, 7)
        cnt = [0]

        def nxt():
            i = cnt[0] % 8
            cnt[0] += 1
            return i

        brs = [int(c) for c in os.environ.get("MK_NSA_BR", "012")]

        def evac(tq, g, br, width, first_branch):
            first_branch = (br == brs[0])
            for i in range(4):
                T_ = 4 * tq + i
                Ov = ps[OB[i]][:, 0:4 * width].rearrange("p (r n) -> p r n", r=4)
                P.op("dve", lambda e, Ov=Ov: e.tensor_scalar(out=den[:, 0:4], in0=Ov[:, :, 64], scalar1=1e-30, scalar2=None,
                                                            op0=ALU.max), reads=[("ps", OB[i])], writes=["den"])
                P.op("dve", lambda e: e.reciprocal(out=rec[:, 0:4], in_=den[:, 0:4]), reads=["den"], writes=["rec"])
                P.op("dve", lambda e, T_=T_: e.tensor_tensor(out=scl[:, 0:4], in0=rec[:, 0:4],
                                                             in1=gates[:, T_, br * 16 + 4 * g:br * 16 + 4 * g + 4], op=ALU.mult),
                     reads=["rec", "gates"], writes=["scl"])
                for r in range(4):
                    if br not in brs:
                        continue
                    if first_branch:
                        P.op("dve", lambda e, Ov=Ov, r=r, i=i: e.tensor_scalar(
                            out=oacc[:, i, r * 64:(r + 1) * 64], in0=Ov[:, r, 0:64], scalar1=scl[:, r:r + 1], scalar2=None,
                            op0=ALU.mult), reads=[("ps", OB[i]), "scl"], writes=[("oacc", i)])
                    else:
                        P.op("dve", lambda e, Ov=Ov, r=r, i=i: e.scalar_tensor_tensor(
                            out=oacc[:, i, r * 64:(r + 1) * 64], in0=Ov[:, r, 0:64], scalar=scl[:, r:r + 1],
                            in1=oacc[:, i, r * 64:(r + 1) * 64], op0=ALU.mult, op1=ALU.add),
                            reads=[("ps", OB[i]), "scl", ("oacc", i)], writes=[("oacc", i)])
                if br == 0:
                    for r in range(4):
                        if r == 0:
                            P.op("dve", lambda e, Ov=Ov: e.tensor_scalar(out=imp[:, 0:32], in0=Ov[:, 0, 65:97],
                                                                        scalar1=rec[:, 0:1], scalar2=None, op0=ALU.mult),
                                 reads=[("ps", OB[i]), "rec"], writes=["imp"])
                        else:
                            P.op("dve", lambda e, Ov=Ov, r=r: e.scalar_tensor_tensor(
                                out=imp[:, 0:32], in0=Ov[:, r, 65:97], scalar=rec[:, r:r + 1], in1=imp[:, 0:32],
                                op0=ALU.mult, op1=ALU.add), reads=[("ps", OB[i]), "rec", "imp"], writes=["imp"])
                    P.op("dve", lambda e, T_=T_: e.tensor_tensor(out=sc2[:, 0:32], in0=imp[:, 0:32], in1=m1[:, T_, :], op=ALU.mult),
                         reads=["imp", "m1"], writes=["sc2"])
                    P.op("dve", lambda e, T_=T_: e.tensor_tensor(out=sc2[:, 0:32], in0=sc2[:, 0:32], in1=m2[:, T_, :], op=ALU.add),
                         reads=["sc2", "m2"], writes=["sc2"])
                    P.op("dve", lambda e: e.max(out=m8[:, 0:8], in_=sc2[:, 0:32]), reads=["sc2"], writes=["m8"])
                    P.op("dve", lambda e: e.tensor_scalar(out=nsel[:, 0:32], in0=sc2[:, 0:32], scalar1=m8[:, 7:8], scalar2=None,
                                                          op0=ALU.is_lt), reads=["sc2", "m8"], writes=["nsel"])
                    sbk = SB_[i]
                    pst = ps[sbk][:, :].bitcast(BF16)
                    P.op("pe", lambda e, pst=pst: e.transpose(pst[0:32, 0:128], nsel[:, 0:32], ident[:]),
                         reads=["nsel", "ident"], writes=[("ps", sbk)])
                    P.op("act", lambda e, pst=pst, i=i: e.activation(out=nselT[0:32, i * 128:(i + 1) * 128], in_=pst[0:32, 0:128],
                                                                    func=AF.Copy), reads=[("ps", sbk)], writes=["nselT"])

        tmp4s = [cv(o + 1024 * b_, 1024, F32, "p (r n) -> p r n", r=4) for b_ in range(2)]; o += 2048
        den4 = cv(o, 64, F32); o += 64
        rec4 = cv(o, 64, F32); o += 64
        scl4 = cv(o, 64, F32); o += 64
        imp4 = cv(o, 512, F32); o += 512
        sc4 = cv(o, 512, F32); o += 512
        m84 = cv(o, 128, F32); o += 128
        nsel4 = cv(o, 256, BF16); o += 256
        assert o <= kv["base"], o

        def evac2(tq, g, br, width):
            Ovs = [ps[OB[i]][:, 0:4 * width].rearrange("p (r n) -> p r n", r=4) for i in range(4)]
            R4 = lambda t_, i: t_[:, 4 * i:4 * i + 4]
            for i in range(4):
                if br == 0:
                    P.op("dve", lambda e, i=i: e.tensor_scalar(out=R4(den4, i), in0=Ovs[i][:, :, 64], scalar1=1e-30, scalar2=None,
                                                              op0=ALU.max), reads=[("ps", OB[i])], writes=[("den", i)])
                else:
                    P.op("dve", lambda e, i=i: e.reciprocal(out=R4(rec4, i), in_=Ovs[i][:, :, 64]), reads=[("ps", OB[i])],
                         writes=[("rec", i)])
            if br == 0:
                for i in range(4):
                    P.op("dve", lambda e, i=i: e.reciprocal(out=R4(rec4, i), in_=R4(den4, i)), reads=[("den", i)], writes=[("rec", i)])
            for i in range(4):
                T_ = 4 * tq + i
                P.op("dve", lambda e, i=i, T_=T_: e.tensor_tensor(out=R4(scl4, i), in0=R4(rec4, i),
                                                                  in1=gates[:, T_, br * 16 + 4 * g:br * 16 + 4 * g + 4], op=ALU.mult),
                     reads=[("rec", i), "gates"], writes=[("scl", i)])
            for i in range(4):
                oav = oacc[:, i, :].rearrange("p (r d) -> p r d", r=4)
                sclb = R4(scl4, i).unsqueeze(2).to_broadcast([128, 4, 64])
                if br == 0:
                    P.op("dve", lambda e, i=i, oav=oav, sclb=sclb: e.tensor_tensor(out=oav, in0=Ovs[i][:, :, 0:64], in1=sclb, op=ALU.mult),
                         reads=[("ps", OB[i]), ("scl", i)], writes=[("oacc", i)])
                else:
                    t4 = tmp4s[i % 2][:, :, 0:64]
                    P.op("dve", lambda e, i=i, sclb=sclb, t4=t4: e.tensor_tensor(out=t4, in0=Ovs[i][:, :, 0:64], in1=sclb, op=ALU.mult),
                         reads=[("ps", OB[i]), ("scl", i)], writes=[("tmp4", i % 2)])
                    P.op("pool", lambda e, oav=oav, t4=t4: e.tensor_tensor(out=oav, in0=oav, in1=t4, op=ALU.add),
                         reads=[("tmp4", i % 2), ("oacc", i)], writes=[("oacc", i)])
            if br == 0:
                I32 = lambda t_, i: t_[:, 32 * i:32 * i + 32]
                for i in range(4):
                    recb = R4(rec4, i).unsqueeze(2).to_broadcast([128, 4, 32])
                    t4 = tmp4s[i % 2][:, :, 0:32]
                    P.op("dve", lambda e, i=i, recb=recb, t4=t4: e.tensor_tensor(out=t4, in0=Ovs[i][:, :, 65:97], in1=recb, op=ALU.mult),
                         reads=[("ps", OB[i]), ("rec", i)], writes=[("tmp4", i % 2)])
                    P.op("dve", lambda e, i=i, t4=t4: e.tensor_reduce(out=I32(imp4, i), in_=t4.rearrange("p r j -> p j r"), axis=AX.X,
                                                                      op=ALU.add), reads=[("tmp4", i % 2)], writes=[("imp", i)])
                for i in range(4):
                    T_ = 4 * tq + i
                    P.op("dve", lambda e, i=i, T_=T_: e.tensor_tensor(out=I32(sc4, i), in0=I32(imp4, i), in1=m1[:, T_, :], op=ALU.mult),
                         reads=[("imp", i), "m1"], writes=[("sc2", i)])
                for i in range(4):
                    T_ = 4 * tq + i
                    P.op("dve", lambda e, i=i, T_=T_: e.tensor_tensor(out=I32(sc4, i), in0=I32(sc4, i), in1=m2[:, T_, :], op=ALU.add),
                         reads=[("sc2", i), "m2"], writes=[("sc2", i)])
                for i in range(4):
                    P.op("dve", lambda e, i=i: e.max(out=m84[:, 8 * i:8 * i + 8], in_=I32(sc4, i)), reads=[("sc2", i)], writes=[("m8", i)])
                for i in range(4):
                    P.op("dve", lambda e, i=i: e.tensor_scalar(out=I32(nsel4, i), in0=I32(sc4, i), scalar1=m84[:, 8 * i + 7:8 * i + 8],
                                                              scalar2=None, op0=ALU.is_lt), reads=[("sc2", i), ("m8", i)], writes=[("nsel", i)])

        def do_group(tq, g, early_hook=None):
            half = slice(64 * (g % 2), 64 * (g % 2) + 64)
            kp = g // 2
            its = [(0, 0)]
            for blk in range(max(0, 4 * tq - 4), 4 * tq + 4):
                its.append((2, blk))
            for blk in range(4 * tq + 4):
                its.append((1, blk))
            started = {0: [False] * 4, 1: [False] * 4, 2: [False] * 4}
            base = cnt[0]
            cnt[0] += len(its)

            def stA(k):
                for ph in range(5):
                    for r in range(4):
                        stA_r(k, r, ph)

            def stB(k):
                for r in range(4):
                    stB_r(k, r)

            def stA_r(k, r, ph):
                br, blk = its[k]
                n = base + k
                sbk = SB_[r]
                pi = 4 * (n % 3) + r
                slot = 4 * kp + r
                if br == 0:
                    N = 512
                    if ph == 0:
                        P.op("pe", lambda e: e.matmul(ps[sbk][:, :], lhsT=kv["kcT"][half, kp, :], rhs=qT[half, slot, tq * 512:(tq + 1) * 512],
                                                      start=True, stop=False, skip_group_check=True), reads=["kcT", "qT"], writes=[("ps", sbk)])
                    elif ph == 1:
                        P.op("pe", lambda e: e.matmul(ps[sbk][:, :], lhsT=ident[:], rhs=cmask[:, tq * 512:(tq + 1) * 512], start=False, stop=True,
                                                      skip_group_check=True), reads=["ident", "c_cmask"], writes=[("ps", sbk)])
                elif br == 1:
                    t0 = max(512 * tq, 128 * blk)
                    N = 512 * (tq + 1) - t0
                    c0 = t0 - 512 * tq
                    diag = blk >= 4 * tq
                    use_sel = tq > 0
                    if ph == 0:
                        P.op("pe", lambda e: e.matmul(ps[sbk][:, 0:N], lhsT=kv["ksT"][half, kp, blk * 128:(blk + 1) * 128],
                                                      rhs=qT[half, slot, t0:t0 + N], start=True, stop=(not use_sel and not diag),
                                                      skip_group_check=True),
                             reads=["ksT", "qT"], writes=[("ps", sbk)])
                    elif ph == 1 and use_sel:
                        P.op("pe", lambda e: e.matmul(ps[sbk][:, 0:N], lhsT=esel[0:32, blk * 128:(blk + 1) * 128], rhs=nselT[0:32, c0:c0 + N],
                                                      start=False, stop=(not diag), skip_group_check=True),
                             reads=["c_esel", "nselT"], writes=[("ps", sbk)])
                    elif ph == 2 and diag:
                        P.op("pe", lambda e: e.matmul(ps[sbk][:, 0:128], lhsT=ident[:], rhs=nsneg[:], start=False, stop=True,
                                                      skip_group_check=True), reads=["ident", "c_nsneg"], writes=[("ps", sbk)])
                else:
                    Tlo = max(blk, 4 * tq)
                    Thi = min(blk + 4, 4 * tq + 3)
                    N = 128 * (Thi - Tlo + 1)
                    t0 = 128 * Tlo
                    dmask = (Tlo == blk)
                    fmask = (Thi == blk + 4)
                    if ph == 0:
                        P.op("pe", lambda e: e.matmul(ps[sbk][:, 0:N], lhsT=kv["kwT"][half, kp, blk * 128:(blk + 1) * 128],
                                                      rhs=qT[half, slot, t0:t0 + N], start=True, stop=(not dmask and not fmask),
                                                      skip_group_check=True), reads=["kwT", "qT"], writes=[("ps", sbk)])
                    elif ph == 2 and dmask:
                        P.op("pe", lambda e: e.matmul(ps[sbk][:, 0:128], lhsT=ident[:], rhs=nsneg[:], start=False, stop=(not fmask),
                                                      skip_group_check=True), reads=["ident", "c_nsneg"], writes=[("ps", sbk)])
                    elif ph == 3 and fmask:
                        P.op("pe", lambda e: e.matmul(ps[sbk][:, N - 128:N], lhsT=ident[:], rhs=farneg[:], start=False, stop=True,
                                                      skip_group_check=True), reads=["ident", "c_farneg"], writes=[("ps", sbk)])
                if ph == 4:
                    P.op("act", lambda e: e.activation(out=PT[pi][:, 0:N], in_=ps[sbk][:, 0:N], func=AF.Exp),
                         reads=[("ps", sbk)], writes=[("PT", pi)])

            def stB_r(k, r):
                br, blk = its[k]
                n = base + k
                pi = 4 * (n % 3) + r
                st_ = started[br]
                if br == 0:
                    for i in range(4):
                        P.op("pe", lambda e, i=i, s0=(not st_[i]): e.matmul(
                            ps[OB[i]][:, r * 97:(r + 1) * 97], lhsT=PT[pi][:, i * 128:(i + 1) * 128], rhs=kv["vc"][:, g, :],
                            start=s0, stop=(r == 3), skip_group_check=True), reads=[("PT", pi), "vc"], writes=[("ps", OB[i])])
                        st_[i] = True
                    if r == 3:
                        evac2(tq, g, 0, 97)
                elif br == 1:
                    t0 = max(512 * tq, 128 * blk)
                    c0 = t0 - 512 * tq
                    nblk = 4 * tq + 4
                    for i in range(c0 // 128, 4):
                        lo = 128 * i - c0
                        lastw = (r == 3) and (blk == min(4 * tq + i, nblk - 1))
                        P.op("pe", lambda e, i=i, lo=lo, s0=(not st_[i]), lastw=lastw: e.matmul(
                            ps[OB[i]][:, r * 65:(r + 1) * 65], lhsT=PT[pi][:, lo:lo + 128], rhs=kv["vs"][:, blk, g, :],
                            start=s0, stop=lastw, skip_group_check=True), reads=[("PT", pi), "vs"], writes=[("ps", OB[i])])
                        st_[i] = True
                    if blk == nblk - 1 and r == 3:
                        evac2(tq, g, 1, 65)
                        for i in range(4):
                            P.op("pool", lambda e, i=i: e.tensor_copy(out=obf[:, i, g * 256:(g + 1) * 256], in_=oacc[:, i, :]),
                                 reads=[("oacc", i)], writes=[("obf", i)])
                else:
                    Tlo = max(blk, 4 * tq)
                    Thi = min(blk + 4, 4 * tq + 3)
                    i0, i1 = Tlo - 4 * tq, Thi - 4 * tq
                    for i in range(i0, i1 + 1):
                        lo = 128 * (i - i0)
                        lastw = (r == 3) and (blk == 4 * tq + i)
                        P.op("pe", lambda e, i=i, lo=lo, s0=(not st_[i]), lastw=lastw: e.matmul(
                            ps[OB[i]][:, r * 65:(r + 1) * 65], lhsT=PT[pi][:, lo:lo + 128], rhs=kv["vw"][:, blk, g, :],
                            start=s0, stop=lastw, skip_group_check=True), reads=[("PT", pi), "vw"], writes=[("ps", OB[i])])
                        st_[i] = True
                    if blk == 4 * tq + 3 and r == 3:
                        evac2(tq, g, 2, 65)

            def sel_transposes():
                sb0 = SB_[0]
                pst = ps[sb0][:, :].bitcast(BF16)
                for i in range(4):
                    P.op("pe", lambda e, i=i: e.transpose(pst[0:32, i * 128:(i + 1) * 128], nsel4[:, 32 * i:32 * i + 32], ident[:]),
                         reads=[("nsel", i), "ident"], writes=[("ps", sb0)])
                P.op("act", lambda e: e.activation(out=nselT[0:32, 0:512], in_=pst[0:32, 0:512], func=AF.Copy),
                     reads=[("ps", sb0)], writes=["nselT"])

            nit = len(its)
            SK = 2
            first_sel = next(i_ for i_, it_ in enumerate(its) if it_[0] == 1)
            for k in range(nit + SK):
                if k == first_sel and tq > 0:
                    sel_transposes()
                if k < nit:
                    stA(k)
                if k - SK >= 0:
                    stB(k - SK)
                if k == 3 and early_hook is not None:
                    early_hook()

        def o_transposes(tq):
            for i in range(4):
                T_ = 4 * tq + i
                sbk = SB_[i]
                pst = ps[sbk][:, :].bitcast(BF16).rearrange("p (k t) -> p k t", k=NK)
                for c in range(NK):
                    P.op("pe", lambda e, c=c, i=i, pst=pst: e.transpose(pst[:, c, :], obf[:, i, c * 128:(c + 1) * 128], ident[:]),
                         reads=[("obf", i), "ident"], writes=[("ps", sbk)])
                P.op("act", lambda e, T_=T_, pst=pst: e.activation(out=oT[:, :, T_ * 128:(T_ + 1) * 128], in_=pst, func=AF.Copy),
                     reads=[("ps", sbk)], writes=[("oT", T_)])

        for tq in range(4):
            for g in range(4):
                hook = (lambda tqp=tq - 1: o_transposes(tqp)) if (g == 0 and tq > 0) else None
                do_group(tq, g, hook)
        o_transposes(3)
        P.barrier()
        self.dbg_reg["oT"] = oT[:, :, :]
        if self.nsa_stop == 2:
            return
        wo = cv(0, 16384, BF16, "p (c n) -> p c n", c=NK)
        wo_v = w_out.rearrange("(c p) n -> p c n", p=128)
        for kq in range(2):
            P.dma("pool", lambda e, kq=kq: e.dma_start(out=wo[:, 4 * kq:4 * kq + 4, :], in_=wo_v[:, 4 * kq:4 * kq + 4, :]),
                  writes=["wo_n"])
        for tt in range(NT):
            for nh in range(2):
                bank = 2 * (tt % 2) + nh
                for c in range(NK):
                    P.op("pe", lambda e, bank=bank, c=c, tt=tt, nh=nh: e.matmul(
                        ps[bank][:, :], lhsT=oT[:, c, tt * 128:(tt + 1) * 128], rhs=wo[:, c, nh * 512:(nh + 1) * 512],
                        start=(c == 0), stop=(c == NK - 1)), reads=[("oT", tt), "wo_n"], writes=[("ps", bank)])
                P.op("dve", lambda e, bank=bank, tt=tt, nh=nh: e.tensor_tensor(
                    out=self.h[:, tt, nh * 512:(nh + 1) * 512], in0=ps[bank][:, :], in1=self.h[:, tt, nh * 512:(nh + 1) * 512],
                    op=ALU.add), reads=[("ps", bank), ("h", tt)], writes=[("h", tt)])
        P.barrier()

    def ffn_prefetch(self, w_in, w_out):
        self.ffn(w_in, w_out, prefetch_only=True)

    def ffn(self, w_in, w_out, prefetch_only=False):
        P = self.P
        C = 4
        wa = [self.carve(b * 8192, 8192, BF16, "p (k n) -> p k n", k=NK) for b in range(2)]
        wb = [self.carve(16384 + b * 8192, 8192, BF16, "p (k n) -> p k n", k=NK) for b in range(2)]
        wo = [self.carve(32768 + b * 8192, 8192, BF16, "p (j n) -> p j n", j=C) for b in range(2)]
        actT = self.carve(49152, 16384, BF16, "p (j t) -> p j t", j=C)
        sa = [self.carve(65536 + b * 2048, 2048, F32) for b in range(2)]
        w_in_v = w_in.rearrange("(k p) n -> p k n", p=128)
        w_out_v = w_out.rearrange("(j p) n -> p j n", p=128)
        groups = [(f0, min(C, NFC - f0)) for f0 in range(0, NFC, C)]

        def load(gi):
            f0, c = groups[gi]
            b = gi % 2
            for half in range(2):
                ks = slice(4 * half, 4 * half + 4)
                P.dma("pool", lambda e, b=b, ks=ks, f0=f0, c=c: e.dma_start(
                    out=wa[b][:, ks, 0:c * 128], in_=w_in_v[:, ks, f0 * 128:(f0 + c) * 128]), writes=[("wa", b)])
                P.dma("pool", lambda e, b=b, ks=ks, f0=f0, c=c: e.dma_start(
                    out=wb[b][:, ks, 0:c * 128], in_=w_in_v[:, ks, FF + f0 * 128:FF + (f0 + c) * 128]),
                    writes=[("wb", b)])
            P.dma("pool", lambda e, b=b, f0=f0, c=c: e.dma_start(
                out=wo[b][:, 0:c, :], in_=w_out_v[:, f0:f0 + c, :]), writes=[("wo", b)])

        def second(gi, tc):
            f0, c = groups[gi]
            b = gi % 2
            for tt in range(4 * tc, 4 * tc + 4):
                yi = tt % 2
                for nh in range(2):
                    bank = 4 + 2 * yi + nh
                    for j in range(c):
                        P.op("pe", lambda e, bank=bank, j=j, tt=tt, nh=nh, b=b, c=c: e.matmul(
                            self.ps[bank][:, :], lhsT=actT[:, j, tt * 128:(tt + 1) * 128],
                            rhs=wo[b][:, j, nh * 512:(nh + 1) * 512], start=(j == 0), stop=(j == c - 1)),
                            reads=[("actT", j, tc), ("wo", b)], writes=[("ps", bank)])
                    P.op("dve", lambda e, bank=bank, tt=tt, nh=nh: e.scalar_tensor_tensor(
                        out=self.h[:, tt, nh * 512:(nh + 1) * 512], in0=self.ps[bank][:, :], scalar=0.5,
                        in1=self.h[:, tt, nh * 512:(nh + 1) * 512], op0=ALU.mult, op1=ALU.add),
                        reads=[("ps", bank), ("h", tt)], writes=[("h", tt)])

        if prefetch_only:
            load(0)
            return
        cnt = 0
        for gi, (f0, c) in enumerate(groups):
            b = gi % 2
            if gi + 1 < len(groups):
                load(gi + 1)
            for tc in range(4):
                for j in range(c):
                    idx = cnt % 2
                    cnt += 1
                    ba, bb = 2 * idx, 2 * idx + 1
                    for (bank, wt, wn) in ((ba, wa, "wa"), (bb, wb, "wb")):
                        for kc in range(NK):
                            P.op("pe", lambda e, bank=bank, wt=wt, kc=kc, j=j, tc=tc, b=b: e.matmul(
                                self.ps[bank][:, :], lhsT=wt[b][:, kc, j * 128:(j + 1) * 128],
                                rhs=self.xT[:, kc, tc * 512:(tc + 1) * 512], start=(kc == 0), stop=(kc == NK - 1)),
                                reads=[(wn, b), ("xT", tc)], writes=[("ps", bank)])
                    P.op("act", lambda e, ba=ba, idx=idx: e.activation(out=sa[idx], in_=self.ps[ba][:, :], func=AF.Silu),
                         reads=[("ps", ba)], writes=[("sa", idx)])
                    P.op("dve", lambda e, bb=bb, idx=idx, j=j, tc=tc: e.tensor_tensor(
                        out=actT[:, j, tc * 512:(tc + 1) * 512], in0=sa[idx], in1=self.ps[bb][:, :], op=ALU.mult),
                        reads=[("sa", idx), ("ps", bb)], writes=[("actT", j, tc)])
                if tc >= 1:
                    second(gi, tc - 1)
            second(gi, 3)


_NC_CACHE = {}


def _get_nc(stages):
    if stages not in _NC_CACHE:
        _NC_CACHE[stages] = Builder(stages).build()
    return _NC_CACHE[stages]


def kernel(**inputs):
    stages = int(os.environ.get("MK_STAGES", "12"))
    nc = _get_nc(stages)
    x = np.asarray(inputs["x"], dtype=np.float32)
    shared = {}
    for k, v in inputs.items():
        if k == "x":
            continue
        a = np.ascontiguousarray(np.asarray(v, dtype=np.float32))
        if k == "kv_norm":
            a = a.reshape(1, D)
        shared[k] = a
    for k, v in _consts().items():
        shared["c_" + k] = v
    ncores = int(os.environ.get("MK_CORES", "8"))
    in_maps = []
    for b in range(ncores):
        m = dict(shared)
        m["x"] = np.ascontiguousarray(x[b])
        in_maps.append(m)
    res = run_bass_kernel_spmd(nc, in_maps, core_ids=list(range(ncores)))
    if os.environ.get("MK_DBG"):
        global DBG_OUT
        DBG_OUT = {k: np.asarray(v) for k, v in res.results[0].items()}
    return np.stack([np.asarray(r["out"]) for r in res.results], axis=0).astype(np.float32)
```
